# Optimizing a Trainium2 kernel written in Bass

```python
import jax, jax.numpy as jnp
from jax import lax
import numpy as np

D_MODEL = 1024
BATCH = 8
SEQ = 2048
DEPTH = 2

GRID_W = 64
CTX_LEN = 256
D_FF = ((8 * D_MODEL // 3 + 255) // 256) * 256
HALF_STEP = 0.5
N_MOD = 9
RMS_EPS = 1e-6
NEG_INF = -1e30

CONV_CH = D_MODEL // 4
CONV_K = 3
POOL_CH = D_MODEL // 4
POOL_WINDOWS = (2, 4, 8, 16)
POOL_GROUP = POOL_CH // 4
NA_HEAD_DIM = 64
NA_CH = D_MODEL // 2
NA_HEADS = NA_CH // NA_HEAD_DIM
NA_KH_MAX = 8
NA_KW = 16
NA_QC = 16
NA_KB = NA_QC + NA_KW - 1
D_MIX = CONV_CH + POOL_CH + NA_CH
OFF_B = CONV_CH
OFF_C = 2 * CONV_CH
OFF_P = 3 * CONV_CH
OFF_Q = 3 * CONV_CH + POOL_CH
OFF_K = OFF_Q + NA_CH
OFF_V = OFF_K + NA_CH
D_IN = OFF_V + NA_CH

kernel_name = 'hybrid_conv_pool_neighbourhood_macaron_dit'


def rms_norm(x, g):
    xf = x.astype(jnp.float32)
    y = xf * lax.rsqrt(jnp.mean(xf * xf, axis=-1, keepdims=True) + RMS_EPS)
    return (y * g.astype(jnp.float32)).astype(x.dtype)


def mod_norm(x, m, i, g):
    return rms_norm(x, g) * (1 + m[:, 3 * i + 1, None]) + m[:, 3 * i, None]


def swiglu(h, w1, w2):
    a, b = jnp.split(h @ w1, 2, axis=-1)
    return (jax.nn.silu(a) * b) @ w2


def ffn_sublayer(x, m, i, g, w1, w2):
    return x + HALF_STEP * m[:, 3 * i + 2, None] * swiglu(mod_norm(x, m, i, g), w1, w2)


def short_conv(u, w):
    return lax.conv_general_dilated(
        u, w[:, None, :].astype(u.dtype), window_strides=(1,),
        padding=[(CONV_K // 2, CONV_K // 2)],
        dimension_numbers=('NWC', 'WIO', 'NWC'), feature_group_count=u.shape[-1])


def gated_conv_mixer(h, b_gate, c_gate, conv_w):
    return b_gate * short_conv(c_gate * h, conv_w)


def multiscale_pool_mixer(v, pool_w, pool_scale):
    L = v.shape[1]
    vf = v.astype(jnp.float32)
    cs = jnp.concatenate([jnp.zeros_like(vf[:, :1]), jnp.cumsum(vf, axis=1)], axis=1)
    t = jnp.arange(L)
    outs = []
    for g, w in enumerate(POOL_WINDOWS):
        left = w // 2
        right = w - 1 - left
        lo = jnp.clip(t - left, 0, L)
        hi = jnp.clip(t + right + 1, 0, L)
        sl = slice(g * POOL_GROUP, (g + 1) * POOL_GROUP)
        mean = (cs[:, hi, sl] - cs[:, lo, sl]) / (hi - lo).astype(jnp.float32)[None, :, None]
        outs.append((mean - vf[..., sl]).astype(v.dtype) @ pool_w[g])
    return jnp.concatenate(outs, axis=-1) * pool_scale


def context_attention(q, k, v):
    B, L = q.shape[:2]
    s = jnp.einsum('bqhd,bkhd->bhqk', q, k, preferred_element_type=jnp.float32) * (NA_HEAD_DIM ** -0.5)
    p = jax.nn.softmax(s, axis=-1).astype(v.dtype)
    return jnp.einsum('bhqk,bkhd->bqhd', p, v).reshape(B, L, NA_CH)


def neighbourhood_attention(q, k, v, k_ctx, v_ctx, rpb):
    B, S = q.shape[:2]
    rows = S // GRID_W
    kh = min(NA_KH_MAX, rows)
    nj = GRID_W // NA_QC
    scale = NA_HEAD_DIM ** -0.5
    qg = q.reshape(B, rows, GRID_W, NA_HEADS, NA_HEAD_DIM)
    kg = k.reshape(B, rows, GRID_W, NA_HEADS, NA_HEAD_DIM)
    vg = v.reshape(B, rows, GRID_W, NA_HEADS, NA_HEAD_DIM)
    qcol = jnp.arange(GRID_W).reshape(nj, NA_QC)
    band0 = jnp.clip(jnp.arange(nj) * NA_QC - NA_KW // 2, 0, GRID_W - NA_KB)
    kcol = band0[:, None] + jnp.arange(NA_KB)
    cstart = jnp.clip(qcol - NA_KW // 2, 0, GRID_W - NA_KW)
    col_valid = (kcol[:, None, :] >= cstart[..., None]) & (kcol[:, None, :] < cstart[..., None] + NA_KW)
    col_off = kcol[:, None, :] - qcol[..., None] + NA_KW - 1
    n_loc = kh * NA_KB

    def row_block(r):
        rs = jnp.clip(r - kh // 2, 0, rows - kh)
        q_r = lax.dynamic_index_in_dim(qg, r, axis=1, keepdims=False).reshape(
            B, nj, NA_QC, NA_HEADS, NA_HEAD_DIM)
        k_b = lax.dynamic_slice_in_dim(kg, rs, kh, axis=1)[:, :, kcol]
        v_b = lax.dynamic_slice_in_dim(vg, rs, kh, axis=1)[:, :, kcol]
        row_off = rs + jnp.arange(kh) - r + NA_KH_MAX - 1
        bias = rpb[:, row_off[None, None, :, None], col_off[:, :, None, :]].astype(jnp.float32)
        bias = jnp.where(col_valid[None, :, :, None, :], bias, NEG_INF)
        s_loc = jnp.einsum('bjqhd,bijkhd->bhjqik', q_r, k_b,
                           preferred_element_type=jnp.float32) * scale + bias
        s_ctx = jnp.einsum('bjqhd,bchd->bhjqc', q_r, k_ctx,
                           preferred_element_type=jnp.float32) * scale
        s = jnp.concatenate([s_loc.reshape(B, NA_HEADS, nj, NA_QC, n_loc), s_ctx], axis=-1)
        p = jax.nn.softmax(s, axis=-1).astype(v.dtype)
        p_loc = p[..., :n_loc].reshape(B, NA_HEADS, nj, NA_QC, kh, NA_KB)
        o = (jnp.einsum('bhjqik,bijkhd->bjqhd', p_loc, v_b)
             + jnp.einsum('bhjqc,bchd->bjqhd', p[..., n_loc:], v_ctx))
        return o.reshape(B, GRID_W, NA_CH)

    out = lax.map(row_block, jnp.arange(rows))
    return out.transpose(1, 0, 2, 3).reshape(B, S, NA_CH)


def heads(t):
    return t.reshape(t.shape[0], t.shape[1], NA_HEADS, NA_HEAD_DIM)


def setup_inputs(seed: int = 0) -> dict:
    key = jax.random.key(seed)
    ks = jax.random.split(key, 16)
    f32 = jnp.float32
    nrm = lambda k, shape, s: jax.random.normal(k, shape, f32) * s
    return {
        'x': nrm(ks[0], (BATCH, SEQ, D_MODEL), 1.0),
        'c': nrm(ks[1], (BATCH, D_MODEL), 1.0),
        'ctx': nrm(ks[2], (BATCH, CTX_LEN, D_MODEL), 1.0),
        'c_ctx': nrm(ks[3], (D_MODEL,), 1.0),
        'w_mod': nrm(ks[4], (DEPTH, D_MODEL, N_MOD * D_MODEL), 0.5 * D_MODEL ** -0.5),
        'b_mod': nrm(ks[5], (DEPTH, N_MOD * D_MODEL), 0.01),
        'norm_g': 1.0 + nrm(ks[6], (DEPTH, 3, D_MODEL), 0.02),
        'ffn_w_in': nrm(ks[7], (DEPTH, 2, D_MODEL, 2 * D_FF), D_MODEL ** -0.5),
        'ffn_w_out': nrm(ks[8], (DEPTH, 2, D_FF, D_MODEL), D_FF ** -0.5),
        'w_in': nrm(ks[9], (DEPTH, D_MODEL, D_IN), D_MODEL ** -0.5),
        'conv_w': nrm(ks[10], (DEPTH, CONV_K, CONV_CH), CONV_K ** -0.5),
        'pool_w': nrm(ks[11], (DEPTH, len(POOL_WINDOWS), POOL_GROUP, POOL_GROUP), POOL_GROUP ** -0.5),
        'pool_scale': 1.0 + nrm(ks[12], (DEPTH, POOL_CH), 0.1),
        'rpb': nrm(ks[13], (DEPTH, NA_HEADS, 2 * NA_KH_MAX - 1, 2 * NA_KW - 1), 0.1),
        'w_out': nrm(ks[14], (DEPTH, D_MIX, D_MODEL), D_MIX ** -0.5),
        'final_g': 1.0 + nrm(ks[15], (D_MODEL,), 0.02),
    }


def reference(x, c, ctx, c_ctx, w_mod, b_mod, norm_g, ffn_w_in, ffn_w_out, w_in,
              conv_w, pool_w, pool_scale, rpb, w_out, final_g):
    B = x.shape[0]
    xc = ctx
    for l in range(DEPTH):
        last = l == DEPTH - 1
        m = (jax.nn.silu(c) @ w_mod[l] + b_mod[l]).reshape(B, N_MOD, D_MODEL)
        mc = (jax.nn.silu(c_ctx) @ w_mod[l] + b_mod[l]).reshape(1, N_MOD, D_MODEL)

        x = ffn_sublayer(x, m, 0, norm_g[l, 0], ffn_w_in[l, 0], ffn_w_out[l, 0])
        xc = ffn_sublayer(xc, mc, 0, norm_g[l, 0], ffn_w_in[l, 0], ffn_w_out[l, 0])

        h = mod_norm(x, m, 1, norm_g[l, 1])
        hc = mod_norm(xc, mc, 1, norm_g[l, 1])
        u = h @ w_in[l]
        if last:
            uc_kv = hc @ w_in[l][:, OFF_K:]
            kc, vc = heads(uc_kv[..., :NA_CH]), heads(uc_kv[..., NA_CH:])
        else:
            uc = hc @ w_in[l]
            kc, vc = heads(uc[..., OFF_K:OFF_V]), heads(uc[..., OFF_V:])
            yc = jnp.concatenate([
                gated_conv_mixer(uc[..., :OFF_B], uc[..., OFF_B:OFF_C], uc[..., OFF_C:OFF_P], conv_w[l]),
                multiscale_pool_mixer(uc[..., OFF_P:OFF_Q], pool_w[l], pool_scale[l]),
                context_attention(heads(uc[..., OFF_Q:OFF_K]), kc, vc),
            ], axis=-1) @ w_out[l]
            xc = xc + mc[:, 5, None] * yc
            xc = ffn_sublayer(xc, mc, 2, norm_g[l, 2], ffn_w_in[l, 1], ffn_w_out[l, 1])
        y = jnp.concatenate([
            gated_conv_mixer(u[..., :OFF_B], u[..., OFF_B:OFF_C], u[..., OFF_C:OFF_P], conv_w[l]),
            multiscale_pool_mixer(u[..., OFF_P:OFF_Q], pool_w[l], pool_scale[l]),
            neighbourhood_attention(heads(u[..., OFF_Q:OFF_K]), heads(u[..., OFF_K:OFF_V]),
                                    heads(u[..., OFF_V:]), kc, vc, rpb[l]),
        ], axis=-1) @ w_out[l]
        x = x + m[:, 5, None] * y

        x = ffn_sublayer(x, m, 2, norm_g[l, 2], ffn_w_in[l, 1], ffn_w_out[l, 1])
    return rms_norm(x, final_g)
```

```python
import numpy as np
from contextlib import ExitStack
import concourse.bass as bass
import concourse.mybir as mybir
from concourse.bass_utils import run_bass_kernel_spmd

F32 = mybir.dt.float32
BF16 = mybir.dt.bfloat16
AF = mybir.ActivationFunctionType
ALU = mybir.AluOpType

D = 1024
S = 2048
CTX = 256
T = S + CTX
DEPTH = 2
DFF = 2816
NJ = DFF // 128
NEG = -30000.0
POOL_WINDOWS = (2, 4, 8, 16)
TBS = [(0, 512), (512, 512), (1024, 512), (1536, 512), (2048, 256)]
GROUPS = [(0, 6), (6, 12), (12, 17), (17, 22)]

P_C2 = 0
P_LAYER = 16
P_LSZ = 72 + 24 + 6 + 2
P_FG = P_LAYER + 2 * P_LSZ
P_INVW = P_FG + 8
P_EC = P_INVW + 2
NPAR = P_EC + 32
C_ID = 0
C_ONES = 128
C_RM = 256
NCB = 256 + 4 * 256
RM_IDX = {0: 0, 1: 1, 4: 2, 5: 3}


def _host_shared(inp):
    f = lambda a: np.ascontiguousarray(a, dtype=np.float32)
    w_in1 = np.asarray(inp["ffn_w_in"], np.float32)
    W1 = f(w_in1.reshape(2, 2, 8, 128, 2, NJ, 128).transpose(0, 1, 5, 3, 2, 4, 6).reshape(2, 2, NJ, 128, 2048))
    W2 = f(np.asarray(inp["ffn_w_out"], np.float32).reshape(2, 2, NJ, 128, 1024))
    WIN = f(np.asarray(inp["w_in"], np.float32).reshape(2, 8, 128, 20, 128).transpose(0, 3, 2, 1, 4).reshape(2, 20, 128, 1024))
    WOUT = f(np.asarray(inp["w_out"], np.float32).reshape(2, 2, 4, 128, 1024).transpose(0, 1, 3, 2, 4).reshape(2, 2, 128, 4096))
    WMOD = f(np.asarray(inp["w_mod"], np.float32).reshape(2, 8, 128, 9, 4, 256).transpose(0, 3, 4, 2, 1, 5).reshape(2, 9, 4, 128, 2048))
    pw = np.asarray(inp["pool_w"], np.float32)
    PBD = np.zeros((2, 2, 128, 128), np.float32)
    for l in range(2):
        for cc in range(2):
            for g in range(2):
                PBD[l, cc, g * 64:(g + 1) * 64, g * 64:(g + 1) * 64] = pw[l, 2 * cc + g]
    rpb = np.asarray(inp["rpb"], np.float32)
    e = np.arange(2)[:, None, None, None]
    kc = np.arange(64)[None, :, None, None]
    s = np.arange(14)[None, None, :, None]
    qc = np.arange(64)[None, None, None, :]
    dr = np.broadcast_to(13 - s + e, (2, 64, 14, 64))
    dc = np.broadcast_to(np.clip(kc - qc + 15, 0, 30), (2, 64, 14, 64))
    cstart = np.clip(qc - 8, 0, 48)
    valid = np.broadcast_to((kc >= cstart) & (kc < cstart + 16), (2, 64, 14, 64))
    vals = rpb[:, :, dr, dc]
    WT_full = np.where(valid[None, None], vals, np.float32(NEG)).astype(np.float32).reshape(2, 8, 1, 128, 896)
    rowok = (dr >= 3) & (dr <= 10)
    WT_int = np.where((valid & rowok)[None, None], vals, np.float32(NEG)).astype(np.float32).reshape(2, 8, 1, 128, 896)
    WT = np.concatenate([WT_full, WT_int], axis=2)
    CB = np.zeros((128, NCB), np.float32)
    CB[:, C_ID:C_ID + 128] = np.eye(128, dtype=np.float32)
    CB[:, C_ONES:C_ONES + 128] = 1.0
    for cc, idx in RM_IDX.items():
        ee = np.arange(2)[:, None, None, None]
        ii = np.arange(4)[None, None, :, None]
        drr = np.broadcast_to(2 * cc + ee - ii + 3, (2, 64, 4, 64))
        CB[:, C_RM + idx * 256:C_RM + (idx + 1) * 256] = np.where((drr >= 3) & (drr <= 10), 0.0, NEG).reshape(128, 256)
    return dict(w1=W1, w2=W2, win=WIN, wout=WOUT, wmod=WMOD, pbd=PBD, wt=f(WT), cb=CB)


def _host_params(inp, b):
    P = np.zeros((128, NPAR), np.float32)
    cm = lambda v: np.asarray(v, np.float32).reshape(8, 128).T
    c2 = np.stack([cm(inp["c"][b]), cm(inp["c_ctx"])], axis=-1)
    P[:, P_C2:P_C2 + 16] = c2.reshape(128, 16)
    for l in range(2):
        o = P_LAYER + l * P_LSZ
        bm = np.asarray(inp["b_mod"], np.float32)[l].reshape(9, 8, 128).transpose(2, 0, 1)
        P[:, o:o + 72] = bm.reshape(128, 72)
        ng = np.asarray(inp["norm_g"], np.float32)[l].reshape(3, 8, 128).transpose(2, 0, 1)
        P[:, o + 72:o + 96] = ng.reshape(128, 24)
        cw = np.asarray(inp["conv_w"], np.float32)[l].reshape(3, 2, 128).transpose(2, 0, 1)
        P[:, o + 96:o + 102] = cw.reshape(128, 6)
        P[:, o + 102:o + 104] = np.asarray(inp["pool_scale"], np.float32)[l].reshape(2, 128).T
    P[:, P_FG:P_FG + 8] = cm(inp["final_g"])
    for cc in range(2):
        for half in range(2):
            w = POOL_WINDOWS[2 * cc + half]
            left = w // 2
            right = w - 1 - left
            rows = slice(half * 64, (half + 1) * 64)
            P[rows, P_INVW + cc] = 1.0 / w
            for t in range(8):
                cnt = t + right + 1 - max(t - left, 0)
                P[rows, P_EC + cc * 16 + t] = w / min(cnt, w)
                cnt2 = min(right + 1, 8 - t) + left
                P[rows, P_EC + cc * 16 + 8 + t] = w / min(cnt2, w)
    return P


def _host_xt(inp, b):
    xa = np.concatenate([np.asarray(inp["x"][b], np.float32), np.asarray(inp["ctx"][b], np.float32)], axis=0)
    return np.ascontiguousarray(xa.T.reshape(8, 128, T).transpose(1, 0, 2))


class Buf:
    __slots__ = ("w", "r", "sem", "semv", "name")

    def __init__(self, name=""):
        self.w = None
        self.r = {}
        self.sem = None
        self.semv = 0
        self.name = name


class Eng:
    def __init__(self, name, sem):
        self.name = name
        self.sem = sem
        self.n = 0
        self.ops = []
        self.waited = {}

    def wait(self, tok):
        if tok is None:
            return
        s, v = tok
        k = id(s)
        if self.waited.get(k, 0) >= v:
            return
        self.waited[k] = v
        self.ops.append(lambda e, s=s, v=v: e.wait_ge(s, v))


class Prog:
    def __init__(self, nc, es):
        self.nc = nc
        self.es = es
        mk = lambda n: Eng(n, es.enter_context(nc.semaphore("sem_" + n)))
        self.pe, self.act, self.dve, self.pool, self.sp = mk("pe"), mk("act"), mk("dve"), mk("pool"), mk("sp")
        self.nsem = 0
        self.phase_toks = []

    def _deps(self, eng, ins, outs):
        for b in ins:
            eng.wait(b.w)
        for b in outs:
            if b.r:
                for t in b.r.values():
                    eng.wait(t)
            elif b.w is not None and not (eng is self.pe and b.w[0] is eng.sem):
                eng.wait(b.w)

    def _mark(self, tok, ins, outs):
        for b in ins:
            b.r[id(tok[0])] = tok
        for b in outs:
            b.w = tok
            b.r = {}

    def op(self, eng, fn, ins=(), outs=()):
        self._deps(eng, ins, outs)
        eng.n += 1
        tok = (eng.sem, eng.n)
        eng.ops.append(lambda e, fn=fn, s=eng.sem: fn(e).then_inc(s, 1))
        self._mark(tok, ins, outs)
        return tok

    def dma(self, eng, pairs, ins=(), outs=(), dbuf=None, phase=False):
        if phase:
            for t in self.phase_toks:
                eng.wait(t)
        self._deps(eng, ins, outs)
        if dbuf.sem is None:
            dbuf.sem = self.es.enter_context(self.nc.semaphore("dsem%d" % self.nsem))
            self.nsem += 1
        for (o, i) in pairs:
            dbuf.semv += 16
            eng.ops.append(lambda e, o=o, i=i, s=dbuf.sem: e.dma_start(out=o, in_=i).then_inc(s, 16))
        tok = (dbuf.sem, dbuf.semv)
        self._mark(tok, ins, outs)
        return tok

    def barrier(self):
        engs = [self.pe, self.act, self.dve]
        toks = [(e.sem, e.n) for e in engs if e.n > 0]
        self.phase_toks = toks
        for e in engs:
            for t in toks:
                if t[0] is not e.sem:
                    e.wait(t)


def build(layer_list=(0, 1), do_final=True, stop=None, dbg=False):
    nc = bass.Bass("TRN2", target_bir_lowering=False)
    din = lambda name, shape: nc.dram_tensor(name, shape, F32, kind="ExternalInput").ap()
    XT = din("xt", [128, 8, T])
    PAR = din("par", [128, NPAR])
    CBD = din("cb", [128, NCB])
    W1 = din("w1", [2, 2, NJ, 128, 2048])
    W2 = din("w2", [2, 2, NJ, 128, 1024])
    WIN = din("win", [2, 20, 128, 1024])
    WOUT = din("wout", [2, 2, 128, 4096])
    WMOD = din("wmod", [2, 9, 4, 128, 2048])
    PBD = din("pbd", [2, 2, 128, 128])
    WT = din("wt", [2, 8, 2, 128, 896])
    OUT = nc.dram_tensor("out", [128, 8, S if do_final else T], F32, kind="ExternalOutput").ap()
    DBG = nc.dram_tensor("dbg", [128, 8, T], F32, kind="ExternalOutput").ap() if dbg else None

    with ExitStack() as es:
        sb = lambda name, shape, dt: es.enter_context(nc.sbuf_tensor(name, shape, dt))
        xT = sb("xT", [128, 8, T], F32)
        hT = sb("hT", [128, 8, T], BF16)
        NR = 35008
        R = sb("R", [128, NR], BF16)
        WS = sb("WS", [128, 6144], BF16)
        W2R = sb("W2R", [128, 6144], BF16)
        par = sb("par_sb", [128, NPAR], F32)
        cb = sb("cb_sb", [128, NCB], BF16)
        scT = sb("scT", [128, 8, 2], BF16)
        MODSL = [sb("MODS%d" % l_, [128, 9, 2, 8], F32) for l_ in range(2)]
        DERL = [sb("DER%d" % l_, [128, 3, 2, 3, 8], F32) for l_ in range(2)]
        pbd = sb("pbd_sb", [128, 2, 128], BF16)
        psb = [es.enter_context(nc.psum_tensor("ps%d" % i, [128, 512], F32)) for i in range(8)]
        P = Prog(nc, es)
        pe, act, dve, pool, sp = P.pe, P.act, P.dve, P.pool, P.sp

        def rb(off, n):
            return R[:, off:off + n]

        def rf(off, n):
            return R[:, off:off + 2 * n].bitcast(F32)

        NT0 = NR - 5120
        sq_t = [rb(NT0 + i * 512, 512) for i in range(2)]
        rt_t = rf(NT0 + 1024, 512)
        rstd_t = rf(NT0 + 2048, 512)
        tmp_t = [rf(NT0 + 3072 + i * 1024, 512) for i in range(2)]
        b_sq = [Buf("sq0"), Buf("sq1")]
        b_rt, b_rstd = Buf("rt"), Buf("rstd")
        b_tmp = [Buf("tmp0"), Buf("tmp1")]
        gT = rb(0, 6 * T).rearrange("p (j t) -> p j t", t=T)
        sa_t = [rf(13824 + i * 1024, 512) for i in range(2)]
        b_sa = [Buf("sa0"), Buf("sa1")]
        wm_t = [rb(15872 + i * 2048, 2048).rearrange("p (k c) -> p k c", c=256) for i in range(3)]
        b_wm = [Buf("wm0"), Buf("wm1"), Buf("wm2")]
        yTh = rb(0, 4 * T).rearrange("p (j t) -> p j t", t=T)
        S1o, S2o, S3o, S4o = 9216, 13888, 18560, 23232
        PTo, RDo, WTo = 25536, 27072, 28096
        NPT = 6
        PT = [rb(PTo + i * 512, 512) for i in range(3)] + [rb(S4o + i * 512, 512) for i in range(3)]
        b_PT = [Buf("PT%d" % i) for i in range(NPT)]
        rden = [rf(RDo + i * 512, 256) for i in range(2)]
        b_rden = [Buf("rd0"), Buf("rd1")]
        WTp = rb(WTo, 1792).rearrange("p (h c) -> p h c", c=896)
        b_WTp = Buf("WTp")

        b_x = [[Buf("x%d_%d" % (k, t)) for t in range(5)] for k in range(8)]
        b_h = [Buf("h%d" % t) for t in range(5)]
        b_g = [[Buf("g%d_%d" % (j, t)) for t in range(5)] for j in range(6)]
        b_y = [[Buf("y%d_%d" % (j, t)) for t in range(5)] for j in range(4)]
        b_ps = [Buf("ps%d" % i) for i in range(8)]
        b_ws = [Buf("ws%d" % i) for i in range(6)]
        b_w2 = Buf("w2")
        b_par, b_cb, b_sc, b_pbd = Buf("par"), Buf("cb"), Buf("sc"), Buf("pbd")
        b_modsl = [[Buf("mods%d_%d" % (l_, mi)) for mi in range(9)] for l_ in range(2)]
        b_derl = [[Buf("der%d_%d" % (l_, i_)) for i_ in range(3)] for l_ in range(2)]
        b_hgl = [[Buf("hg%d_%d" % (l_, i_)) for i_ in range(3)] for l_ in range(2)]
        cur = {"l": 0}
        b_out = Buf("out")
        ident = cb[:, C_ID:C_ID + 128]
        ones = cb[:, C_ONES:C_ONES + 128]

        st = {"s": 0, "l": 0}

        def ps_s():
            i = st["s"] % 6
            st["s"] += 1
            return psb[i], b_ps[i]

        def ps_l():
            i = 6 + st["l"] % 2
            st["l"] += 1
            return psb[i], b_ps[i]

        def ws_w1(s_):
            return WS[:, s_ * 2048:(s_ + 1) * 2048].rearrange("p (k c) -> p k c", c=256), [b_ws[2 * s_], b_ws[2 * s_ + 1]]

        def ws_win(s_):
            return WS[:, s_ * 1024:(s_ + 1) * 1024].rearrange("p (k c) -> p k c", c=128), [b_ws[s_]]

        def mm_group(out_ap, pairs, ins, outb, extra_out=()):
            n = len(pairs)

            def fn(e, out_ap=out_ap, pairs=pairs, n=n):
                r = None
                for i, (l_, r_) in enumerate(pairs):
                    r = e.matmul(out_ap, l_, r_, start=(i == 0), stop=(i == n - 1))
                return r
            return P.op(pe, fn, ins=ins, outs=[outb] + list(extra_out))

        if True:
            P.dma(sp, [(par[:], PAR)], outs=[b_par], dbuf=b_par)
            P.dma(pool, [(cb[:], CBD)], outs=[b_cb], dbuf=b_cb)
            def load_x(tb):
                t0, n = TBS[tb]
                bl = [b_x[k][tb] for k in range(8)]
                P.dma(sp, [(xT[:, :, t0:t0 + n], XT[:, :, t0:t0 + n])], outs=bl, dbuf=b_x[0][tb])
            load_x(0)
            P.op(act, lambda e: e.activation(out=scT[:].rearrange("p k w -> p (k w)"), in_=par[:, P_C2:P_C2 + 16], func=AF.Silu),
                 ins=[b_par], outs=[b_sc])

        def pcol(c):
            return par[:, c:c + 1]

        mod_items = []

        def make_mod_items(l):
            lo = P_LAYER + l * P_LSZ
            MODS, DER = MODSL[l], DERL[l]
            state = {}

            def item(mi, q):
                def run():
                    if q == 0:
                        state["ps"] = ps_l()
                    pst, bp = state["ps"]
                    s_ = st.get("wm", 0) % 3
                    st["wm"] = s_ + 1
                    P.dma(pool, [(wm_t[s_].rearrange("p k c -> p (k c)"), WMOD[l, mi, q])], outs=[b_wm[s_]], dbuf=b_wm[s_], phase=True)
                    for fl in range(2):
                        fc = q * 2 + fl
                        pairs = [(wm_t[s_][:, kc, fl * 128:(fl + 1) * 128], scT[:, kc, :]) for kc in range(8)]
                        mm_group(pst[:, fc * 2:fc * 2 + 2], pairs, ins=[b_wm[s_], b_sc], outb=bp)
                    if q == 3:
                        pv = pst[:, 0:16].rearrange("p (f w) -> p f w", w=2)
                        for wh in range(2):
                            P.op(dve, lambda e, wh=wh: e.tensor_tensor(
                                out=MODS[:, mi, wh, :], in0=pv[:, :, wh], in1=par[:, lo + mi * 8:lo + mi * 8 + 8], op=ALU.add),
                                ins=[bp, b_par], outs=[b_modsl[l][mi]])
                        i = mi // 3
                        if mi % 3 == 1:
                            for wh in range(2):
                                g_ap = par[:, lo + 72 + i * 8:lo + 72 + i * 8 + 8]
                                P.op(dve, lambda e, wh=wh, g_ap=g_ap: e.scalar_tensor_tensor(
                                    out=DER[:, i, wh, 0, :], in0=MODS[:, 3 * i + 1, wh, :], scalar=1.0, in1=g_ap, op0=ALU.add, op1=ALU.mult),
                                    ins=[b_modsl[l][3 * i + 1], b_par], outs=[b_derl[l][i]])
                                P.op(dve, lambda e, wh=wh: e.tensor_copy(out=DER[:, i, wh, 1, :], in_=MODS[:, 3 * i, wh, :]),
                                     ins=[b_modsl[l][3 * i]], outs=[b_derl[l][i]])
                        if mi % 3 == 2:
                            for wh in range(2):
                                hs = 1.0 if i == 1 else 0.5
                                P.op(dve, lambda e, wh=wh, hs=hs: e.tensor_scalar(
                                    out=DER[:, i, wh, 2, :], in0=MODS[:, 3 * i + 2, wh, :], scalar1=hs, scalar2=None, op0=ALU.mult),
                                    ins=[b_modsl[l][3 * i + 2]], outs=[b_hgl[l][i]])
                return run
            for mi in range(9):
                for q in range(4):
                    mod_items.append(item(mi, q))

        def pump_mods(n):
            for _ in range(n):
                if mod_items:
                    mod_items.pop(0)()

        def norm(tb, gs_fn, sh_fn, dst_fn, dst_bufs, dep):
            t0, n = TBS[tb]
            xin = [b_x[k][tb] for k in range(8)]
            pst, bp = ps_s()
            for kc in range(8):
                s_ = kc % 2
                P.op(act, lambda e, kc=kc, s_=s_: e.activation(out=sq_t[s_][:, :n], in_=xT[:, kc, t0:t0 + n], func=AF.Square),
                     ins=[b_x[kc][tb]], outs=[b_sq[s_]])

                def fn(e, kc=kc, s_=s_, pst=pst):
                    return e.matmul(pst[:, :n], ones, sq_t[s_][:, :n], start=(kc == 0), stop=(kc == 7))
                P.op(pe, fn, ins=[b_sq[s_], b_cb], outs=[bp])
            P.op(act, lambda e, pst=pst: e.activation(out=rt_t[:, :n], in_=pst[:, :n], func=AF.Sqrt, bias=1e-6, scale=1.0 / D),
                 ins=[bp], outs=[b_rt])
            P.op(dve, lambda e: e.reciprocal(out=rstd_t[:, :n], in_=rt_t[:, :n]), ins=[b_rt], outs=[b_rstd])
            for kc in range(8):
                s_ = kc % 2
                P.op(dve if kc in (1, 4, 7) else pool, lambda e, kc=kc, s_=s_: e.tensor_tensor(
                    out=tmp_t[s_][:, :n], in0=xT[:, kc, t0:t0 + n], in1=rstd_t[:, :n], op=ALU.mult),
                    ins=[b_x[kc][tb], b_rstd], outs=[b_tmp[s_]])
                if sh_fn is None:
                    P.op(act, lambda e, kc=kc, s_=s_: e.activation(
                        out=dst_fn(kc), in_=tmp_t[s_][:, :n], func=AF.Identity, scale=gs_fn(kc)),
                        ins=[b_tmp[s_]] + dep, outs=dst_bufs)
                else:
                    P.op(act, lambda e, kc=kc, s_=s_: e.activation(
                        out=dst_fn(kc), in_=tmp_t[s_][:, :n], func=AF.Identity, bias=sh_fn(kc), scale=gs_fn(kc)),
                        ins=[b_tmp[s_]] + dep, outs=dst_bufs)

        def mod_norm(tb, i, l):
            t0, n = TBS[tb]
            wh = 1 if tb == 4 else 0
            DER = DERL[l]
            norm(tb, lambda kc: DER[:, i, wh, 0, kc:kc + 1], lambda kc: DER[:, i, wh, 1, kc:kc + 1],
                 lambda kc: hT[:, kc, t0:t0 + n], [b_h[tb]], [b_derl[l][i]])

        def ffn(l, f, blocks, pre_normed=False, next_norm=None):
            i = 0 if f == 0 else 2
            P.barrier()
            if not pre_normed:
                for tb in blocks:
                    mod_norm(tb, i, l)
            for gi, (j0, j1) in enumerate(GROUPS):
                ng = j1 - j0
                for j in range(j0, j1):
                    wv, wb = ws_w1(j % 3)
                    P.dma(pool, [(wv.rearrange("p k c -> p (k c)"), W1[l, f, j])], outs=wb, dbuf=wb[0])
                    if j == j0 + 1 or ng == 1:
                        P.dma(pool, [(W2R[:, jl * 1024:(jl + 1) * 1024], W2[l, f, j0 + jl]) for jl in range(ng)],
                              outs=[b_w2], dbuf=b_w2)
                    if j >= 3:
                        pump_mods(2)
                    for tb in blocks:
                        t0, n = TBS[tb]
                        pa, ba = ps_s()
                        pb_, bb = ps_s()
                        mm_group(pa[:, :n], [(wv[:, kc, 0:128], hT[:, kc, t0:t0 + n]) for kc in range(8)], ins=wb + [b_h[tb]], outb=ba)
                        mm_group(pb_[:, :n], [(wv[:, kc, 128:256], hT[:, kc, t0:t0 + n]) for kc in range(8)], ins=wb + [b_h[tb]], outb=bb)
                        ss = st.get("sa", 0) % 2
                        st["sa"] = ss + 1
                        P.op(act, lambda e, pa=pa, ss=ss, n=n: e.activation(out=sa_t[ss][:, :n], in_=pa[:, :n], func=AF.Silu),
                             ins=[ba], outs=[b_sa[ss]])
                        P.op(dve, lambda e, pb_=pb_, ss=ss, n=n, j=j, j0=j0, t0=t0: e.tensor_tensor(
                            out=gT[:, j - j0, t0:t0 + n], in0=sa_t[ss][:, :n], in1=pb_[:, :n], op=ALU.mult),
                            ins=[b_sa[ss], bb], outs=[b_g[j - j0][tb]])
                hook = next_norm if gi == len(GROUPS) - 1 else None
                for bi, tb in enumerate(blocks):
                    t0, n = TBS[tb]
                    wh = 1 if tb == 4 else 0
                    if hook is not None and bi >= 1:
                        hook(blocks[bi - 1])
                    for oc in range(8):
                        po, bo = ps_s()
                        pairs = [(W2R[:, jl * 1024 + oc * 128:jl * 1024 + (oc + 1) * 128], gT[:, jl, t0:t0 + n]) for jl in range(ng)]
                        mm_group(po[:, :n], pairs, ins=[b_w2] + [b_g[jl][tb] for jl in range(ng)], outb=bo)
                        P.op(dve, lambda e, po=po, oc=oc, t0=t0, n=n, wh=wh: e.scalar_tensor_tensor(
                            out=xT[:, oc, t0:t0 + n], in0=po[:, :n], scalar=DERL[l][:, i, wh, 2, oc:oc + 1], in1=xT[:, oc, t0:t0 + n],
                            op0=ALU.mult, op1=ALU.add),
                            ins=[bo, b_hgl[l][i], b_x[oc][tb]], outs=[b_x[oc][tb]])
                if hook is not None:
                    hook(blocks[-1])

        def mixer(l, last, pre_normed=False, next_norm=None):
            lo = P_LAYER + l * P_LSZ
            xb = [0, 1, 2, 3] if last else [0, 1, 2, 3, 4]
            P.barrier()
            if not pre_normed:
                for tb in range(5):
                    mod_norm(tb, 1, l)
            P.dma(pool, [(pbd[:, cc, :], PBD[l, cc]) for cc in range(2)], outs=[b_pbd], dbuf=b_pbd)
            S1, S2, S3 = rf(S1o, 2336), rf(S2o, 2336), rf(S3o, 2336)
            Dt = rb(S4o, 2304)

            def proj_fm(slot, tb):
                t0, n = TBS[tb]
                wv, wb = ws_win(slot)
                pp, bpp = ps_s()
                mm_group(pp[:, :n], [(wv[:, kc, :], hT[:, kc, t0:t0 + n]) for kc in range(8)], ins=wb + [b_h[tb]], outb=bpp)
                return pp, bpp

            def load_win(slot, cch):
                wv, wb = ws_win(slot)
                P.dma(pool, [(wv.rearrange("p k c -> p (k c)"), WIN[l, cch])], outs=wb, dbuf=wb[0])


            b_S1 = [Buf("S1_%d" % t) for t in range(5)]
            b_S2 = [Buf("S2_%d" % t) for t in range(5)]
            b_S3 = [Buf("S3_%d" % t) for t in range(5)]
            b_D = [Buf("D_%d" % t) for t in range(5)]
            for (a_, b_) in ((0, 8), (2056, 2072), (2328, 2336)):
                P.op(dve, lambda e, a_=a_, b_=b_: e.memset(S1[:, a_:b_], 0.0), outs=b_S1)

            def col0(tb):
                return 8 + TBS[tb][0] + (16 if tb == 4 else 0)

            def nb(bl, tb):
                return [bl[t] for t in (tb - 1, tb, tb + 1) if 0 <= t < 5]

            def conv_evac(cc, sl, tb):
                t0, n = TBS[tb]
                a0 = col0(tb)
                ph, bh = proj_fm(sl[0], tb)
                pB, bB = proj_fm(sl[1], tb)
                pC, bC = proj_fm(sl[2], tb)
                ss = tb % 2
                P.op(act, lambda e: e.activation(out=tmp_t[ss][:, :n], in_=pC[:, :n], func=AF.Identity), ins=[bC], outs=[b_tmp[ss]])
                P.op(dve, lambda e: e.tensor_tensor(out=S1[:, a0:a0 + n], in0=ph[:, :n], in1=tmp_t[ss][:, :n], op=ALU.mult),
                     ins=[bh, b_tmp[ss]], outs=[b_S1[tb]])
                P.op(act, lambda e: e.activation(out=S2[:, a0:a0 + n], in_=pB[:, :n], func=AF.Identity), ins=[bB], outs=[b_S2[tb]])

            def conv_tail(cc, tb):
                t0, n = TBS[tb]
                a0 = col0(tb)
                w0, w1, w2 = [pcol(lo + 96 + k * 2 + cc) for k in range(3)]
                acc = S3[:, a0:a0 + n]
                P.op(dve, lambda e: e.tensor_scalar(out=acc, in0=S1[:, a0:a0 + n], scalar1=w1, scalar2=None, op0=ALU.mult),
                     ins=[b_S1[tb], b_par], outs=[b_S3[tb]])
                P.op(dve, lambda e: e.scalar_tensor_tensor(out=acc, in0=S1[:, a0 - 1:a0 - 1 + n], scalar=w0, in1=acc, op0=ALU.mult, op1=ALU.add),
                     ins=nb(b_S1, tb) + [b_S3[tb], b_par], outs=[b_S3[tb]])
                P.op(dve, lambda e: e.scalar_tensor_tensor(out=acc, in0=S1[:, a0 + 1:a0 + 1 + n], scalar=w2, in1=acc, op0=ALU.mult, op1=ALU.add),
                     ins=nb(b_S1, tb) + [b_S3[tb], b_par], outs=[b_S3[tb]])
                P.op(dve, lambda e: e.tensor_tensor(out=yTh[:, cc, t0:t0 + n], in0=acc, in1=S2[:, a0:a0 + n], op=ALU.mult),
                     ins=[b_S3[tb], b_S2[tb]], outs=[b_y[cc][tb]])

            def pool_evac(cc, sl, tb):
                t0, n = TBS[tb]
                a0 = col0(tb)
                pp, bpp = proj_fm(sl[0], tb)
                P.op(act, lambda e: e.activation(out=S1[:, a0:a0 + n], in_=pp[:, :n], func=AF.Identity), ins=[bpp], outs=[b_S1[tb]])

            def pool_tail(cc, tb):
                t0, n = TBS[tb]
                a0 = col0(tb)
                b0 = a0 + n

                def shadd(dst, bd, src, bs, lo_, hi_, sh):
                    P.op(dve, lambda e: e.tensor_tensor(out=dst[:, lo_:hi_], in0=src[:, lo_ + sh:hi_ + sh], in1=src[:, lo_ - sh:hi_ - sh], op=ALU.add),
                         ins=nb(bs, tb), outs=nb(bd, tb))
                P.op(dve, lambda e: e.tensor_tensor(out=S2[:, a0 - 7:b0 + 7], in0=S1[:, a0 - 7:b0 + 7], in1=S1[:, a0 - 8:b0 + 6], op=ALU.add),
                     ins=nb(b_S1, tb), outs=nb(b_S2, tb))
                shadd(S3, b_S3, S2, b_S2, a0 - 6, b0 + 6, 1)
                if cc == 1:
                    shadd(S2, b_S2, S3, b_S3, a0 - 4, b0 + 4, 2)
                    shadd(S3, b_S3, S2, b_S2, a0, b0, 4)
                for half, (Pb, bP) in enumerate(((S2, b_S2), (S3, b_S3))):
                    rows = slice(half * 64, (half + 1) * 64)
                    fixes = []
                    if tb in (0, 4):
                        fixes.append((a0, par[rows, P_EC + cc * 16:P_EC + cc * 16 + 8]))
                    if tb in (3, 4):
                        fixes.append((b0 - 8, par[rows, P_EC + cc * 16 + 8:P_EC + cc * 16 + 16]))
                    for (c_, ecap) in fixes:
                        P.op(dve, lambda e, c_=c_, ecap=ecap, Pb=Pb, rows=rows: e.tensor_tensor(
                            out=Pb[rows, c_:c_ + 8], in0=Pb[rows, c_:c_ + 8], in1=ecap, op=ALU.mult), ins=[bP[tb], b_par], outs=[bP[tb]])
                    P.op(dve, lambda e, Pb=Pb, rows=rows: e.scalar_tensor_tensor(
                        out=Dt[rows, t0:t0 + n], in0=Pb[rows, a0:b0], scalar=par[rows, P_INVW + cc:P_INVW + cc + 1], in1=S1[rows, a0:b0],
                        op0=ALU.mult, op1=ALU.subtract), ins=[bP[tb], b_S1[tb], b_par], outs=[b_D[tb]])
                pp, bpp = ps_s()
                mm_group(pp[:, :n], [(pbd[:, cc, :], Dt[:, t0:t0 + n])], ins=[b_pbd, b_D[tb]], outb=bpp)
                P.op(act, lambda e: e.activation(out=yTh[:, 2 + cc, t0:t0 + n], in_=pp[:, :n], func=AF.Identity, scale=pcol(lo + 102 + cc)),
                     ins=[bpp, b_par], outs=[b_y[2 + cc][tb]])

            units = [(pool_evac, pool_tail, 0, (0,), (6,)), (pool_evac, pool_tail, 1, (1,), (7,)),
                     (conv_evac, conv_tail, 0, (2, 3, 4), (0, 2, 4)), (conv_evac, conv_tail, 1, (5, 0, 1), (1, 3, 5))]
            for (_, _, _, sl, cchs) in units[:3]:
                for s_, c_ in zip(sl, cchs):
                    load_win(s_, c_)
            for u in range(len(units) + 1):
                if u == 2:
                    for s_, c_ in zip(units[3][3], units[3][4]):
                        load_win(s_, c_)
                prev = units[u - 1] if u >= 1 else None
                curu = units[u] if u < len(units) else None
                if prev is not None:
                    prev[1](prev[2], xb[0])
                for bi, tb in enumerate(xb):
                    if prev is not None and bi + 1 < len(xb):
                        prev[1](prev[2], xb[bi + 1])
                    if curu is not None:
                        curu[0](curu[2], curu[3], tb)

            def wout_pass(hf):
                P.dma(pool, [(W2R[:, 0:4096], WOUT[l, hf])], outs=[b_w2], dbuf=b_w2)
                hook = next_norm if hf == 1 else None
                for bi, tb in enumerate(xb):
                    t0, n = TBS[tb]
                    wh = 1 if tb == 4 else 0
                    if hook is not None and bi >= 1:
                        hook(xb[bi - 1])
                    for oc in range(8):
                        po, bo = ps_s()
                        pairs = [(W2R[:, c * 1024 + oc * 128:c * 1024 + (oc + 1) * 128], yTh[:, c, t0:t0 + n]) for c in range(4)]
                        mm_group(po[:, :n], pairs, ins=[b_w2] + [b_y[c][tb] for c in range(4)], outb=bo)
                        P.op(dve, lambda e, po=po, oc=oc, t0=t0, n=n, wh=wh: e.scalar_tensor_tensor(
                            out=xT[:, oc, t0:t0 + n], in0=po[:, :n], scalar=DERL[l][:, 1, wh, 2, oc:oc + 1], in1=xT[:, oc, t0:t0 + n],
                            op0=ALU.mult, op1=ALU.add), ins=[bo, b_hgl[l][1], b_x[oc][tb]], outs=[b_x[oc][tb]])
                if hook is not None:
                    hook(xb[-1])

            if dbg:
                P.dma(pool, [(DBG[:, 0:4, :], yTh[:, :, :])], ins=[b_y[c][t] for c in range(4) for t in range(5)], dbuf=b_out)
            wout_pass(0)

            QT0 = rb(S1o, T)
            KT = rb(S1o + T, T)
            Vp = rb(S2o, 18 * 192).rearrange("p (t c) -> p t c", c=192)
            QT1 = rb(S3o, T)
            WTi = rb(S3o + T, 1792).rearrange("p (h c) -> p h c", c=896)
            QTs = (QT0, QT1)
            b_Q = [Buf("Q%d" % t) for t in range(5)]
            b_K = [Buf("K%d" % t) for t in range(5)]
            b_V = [Buf("V%d" % t) for t in range(5)]
            b_const = Buf("qzero_vones")
            first_pair = True
            for p in range(4):
                sq_, sk_, sv_ = [(2 + 3 * p + i_) % 6 for i_ in range(3)]
                load_win(sq_, 8 + p)
                load_win(sk_, 12 + p)
                load_win(sv_, 16 + p)
                if first_pair:
                    P.barrier()
                    first_pair = False
                    P.op(dve, lambda e: e.memset(QT0[64:128, :], 0.0), outs=[b_const])
                    P.op(dve, lambda e: e.memset(QT1[0:64, :], 0.0), outs=[b_const])
                    P.op(dve, lambda e: e.memset(Vp[:, :, 64:128], 1.0), outs=[b_const])
                P.dma(pool, [(WTp[:, hh, :], WT[l, 2 * p + hh, 0]) for hh in range(2)] + [(WTi[:, hh, :], WT[l, 2 * p + hh, 1]) for hh in range(2)],
                      outs=[b_WTp], dbuf=b_WTp, phase=True)
                for tb in xb:
                    t0, n = TBS[tb]
                    pp, bpp = proj_fm(sq_, tb)
                    P.op(act, lambda e, pp=pp, t0=t0, n=n: e.activation(out=QT0[0:64, t0:t0 + n], in_=pp[0:64, :n], func=AF.Identity, scale=0.125),
                         ins=[bpp], outs=[b_Q[tb]])
                    P.op(act, lambda e, pp=pp, t0=t0, n=n: e.activation(out=QT1[64:128, t0:t0 + n], in_=pp[64:128, :n], func=AF.Identity, scale=0.125),
                         ins=[bpp], outs=[b_Q[tb]])
                for tb in range(5):
                    t0, n = TBS[tb]
                    pp, bpp = proj_fm(sk_, tb)
                    P.op(dve, lambda e, pp=pp, t0=t0, n=n: e.tensor_copy(out=KT[:, t0:t0 + n], in_=pp[:, :n]), ins=[bpp], outs=[b_K[tb]])
                wv, wb = ws_win(sv_)
                for tb in range(5):
                    t0, n = TBS[tb]
                    nt = n // 128
                    pp, bpp = ps_s()
                    for ti in range(nt):
                        tt = t0 // 128 + ti
                        mm_group(pp[:, ti * 128:(ti + 1) * 128],
                                 [(hT[:, kc, tt * 128:(tt + 1) * 128], wv[:, kc, :]) for kc in range(8)], ins=wb + [b_h[tb]], outb=bpp)
                    for hh in range(2):
                        P.op(dve, lambda e, pp=pp, t0=t0, nt=nt, n=n, hh=hh: e.tensor_copy(
                            out=Vp[:, t0 // 128:t0 // 128 + nt, hh * 128:hh * 128 + 64],
                            in_=pp[:, :n].rearrange("p (t c) -> p t c", c=128)[:, :, hh * 64:(hh + 1) * 64]),
                            ins=[bpp], outs=[b_V[tb]])

                qblocks = []
                for m in range(8):
                    if m == 0:
                        ccs = [2, 3, 4, 5]
                    elif m == 7:
                        ccs = [0, 1, 2, 3]
                    else:
                        ccs = [0, 1, 2, 3, 4, 5]
                    interior = (1 <= m <= 6)
                    chunks = [(2 * m - 2 + c, c, interior) for c in ccs] + [(16, None, False), (17, None, False)]
                    qblocks.append((256 * m, chunks))
                if not last:
                    qblocks.append((2048, [(16, None, False), (17, None, False)]))

                LAG = 3
                pending = []

                def flush(keep):
                    while len(pending) > keep:
                        pending.pop(0)()

                for (q0, chunks) in qblocks:
                    qtb = q0 // 512
                    po, bo = ps_l()
                    for hh in range(2):
                        npair = len(chunks) // 2
                        for cp in range(npair):
                            pS, bS = ps_s()
                            for ci in range(2):
                                kci, c, interior = chunks[2 * cp + ci]
                                ktb = (kci * 128) // 512
                                oap = pS[:, ci * 256:(ci + 1) * 256]
                                pairs = [(KT[:, kci * 128:(kci + 1) * 128], QTs[hh][:, q0:q0 + 256])]
                                ins_ = [b_K[ktb], b_Q[qtb], b_const]
                                if c is not None:
                                    Wsel = WTi if interior else WTp
                                    pairs.append((ident, Wsel[:, hh, (10 - 2 * c) * 64:(10 - 2 * c) * 64 + 256]))
                                    ins_ += [b_cb, b_WTp]
                                mm_group(oap, pairs, ins=ins_, outb=bS)
                            pi = st.get("pt", 0) % NPT
                            st["pt"] = pi + 1
                            P.op(act, lambda e, pS=pS, pi=pi: e.activation(out=PT[pi], in_=pS[:, :], func=AF.Exp),
                                 ins=[bS], outs=[b_PT[pi]])

                            def pv_job(cp=cp, pi=pi, hh=hh, chunks=chunks, po=po, bo=bo, npair=npair):
                                kcis = [chunks[2 * cp + ci][0] for ci in range(2)]
                                first_ = (hh == 0 and cp == 0)
                                last_ = (cp == npair - 1)

                                def fn(e):
                                    r = None
                                    for ci in range(2):
                                        r = e.matmul(po[:, hh * 256:(hh + 1) * 256], Vp[:, kcis[ci], hh * 64:hh * 64 + 128],
                                                     PT[pi][:, ci * 256:(ci + 1) * 256], start=(first_ and ci == 0), stop=(last_ and ci == 1),
                                                     skip_group_check=True)
                                    return r
                                P.op(pe, fn, ins=[b_V[(k * 128) // 512] for k in kcis] + [b_PT[pi], b_const], outs=[bo])
                            pending.append(pv_job)
                            flush(LAG)

                    def evac_job(po=po, bo=bo, q0=q0, p=p):
                        ri = st.get("rd", 0) % 2
                        st["rd"] = ri + 1
                        tb_ = q0 // 512
                        P.op(dve, lambda e: e.reciprocal(out=rden[ri][0:64, :], in_=po[64:128, 0:256]), ins=[bo], outs=[b_rden[ri]])
                        P.op(dve, lambda e: e.reciprocal(out=rden[ri][64:128, :], in_=po[0:64, 256:512]), ins=[bo], outs=[b_rden[ri]])
                        P.op(dve, lambda e: e.tensor_tensor(
                            out=yTh[0:64, p, q0:q0 + 256], in0=po[0:64, 0:256], in1=rden[ri][0:64, :], op=ALU.mult),
                            ins=[bo, b_rden[ri]], outs=[b_y[p][tb_]])
                        P.op(dve, lambda e: e.tensor_tensor(
                            out=yTh[64:128, p, q0:q0 + 256], in0=po[64:128, 256:512], in1=rden[ri][64:128, :], op=ALU.mult),
                            ins=[bo, b_rden[ri]], outs=[b_y[p][tb_]])
                    pending.append(evac_job)
                flush(0)
            if dbg:
                P.dma(pool, [(DBG[:, 4:8, :], yTh[:, :, :])], ins=[b_y[c][t] for c in range(4) for t in range(5)], dbuf=b_out)
            wout_pass(1)

        hflat = hT[:, :, :].rearrange("p k t -> p (k t)")
        ot = [hflat[:, i_ * 8192:(i_ + 1) * 8192].bitcast(F32).rearrange("p (k t) -> p k t", t=512) for i_ in range(2)]

        def final_norm(tb):
            t0, n = TBS[tb]
            oi = tb % 2
            norm(tb, lambda kc: par[:, P_FG + kc:P_FG + kc + 1], None, lambda kc: ot[oi][:, kc, :], list(b_h), [b_par])
            P.dma(sp, [(OUT[:, :, t0:t0 + n], ot[oi][:, :, :])], ins=list(b_h), dbuf=b_out)

        make_mod_items(layer_list[0])
        pump_mods(8)
        for s_ in range(3):
            if b_wm[s_].w is not None:
                sp.wait(b_wm[s_].w)
        for tb in range(1, 5):
            load_x(tb)
        nl = len(layer_list)
        stopped = False
        for li, l in enumerate(layer_list):
            last = (l == DEPTH - 1)
            cur["l"] = l
            full = stop is None
            ffn(l, 0, [0, 1, 2, 3, 4], pre_normed=(li > 0),
                next_norm=(lambda tb, l=l: mod_norm(tb, 1, l)) if stop != "ffn1" else None)
            pump_mods(100)
            if stop == "ffn1":
                stopped = True
                break
            blocks2 = [0, 1, 2, 3] if last else [0, 1, 2, 3, 4]
            mixer(l, last, pre_normed=True, next_norm=(lambda tb, l=l: mod_norm(tb, 2, l)) if stop != "mixer" else None)
            if stop == "mixer":
                stopped = True
                break
            if li + 1 < nl:
                make_mod_items(layer_list[li + 1])
                nn = (lambda tb, l2=layer_list[li + 1]: mod_norm(tb, 0, l2))
            elif do_final:
                nn = final_norm
            else:
                nn = None
            ffn(l, 1, blocks2, pre_normed=True, next_norm=nn)
            pump_mods(100)

        if not do_final:
            P.barrier()
            for tb, (t0, n) in enumerate(TBS):
                P.dma(sp, [(OUT[:, :, t0:t0 + n], xT[:, :, t0:t0 + n])], ins=[b_x[k][tb] for k in range(8)], dbuf=b_out)
        sp.wait((b_out.sem, b_out.semv))
        if dbg:
            pool.wait((b_out.sem, b_out.semv))

        with nc.Block() as block:
            @block.tensor
            def _(e):
                for f_ in pe.ops:
                    f_(e)

            @block.scalar
            def _(e):
                for f_ in act.ops:
                    f_(e)

            @block.vector
            def _(e):
                for f_ in dve.ops:
                    f_(e)

            @block.gpsimd
            def _(e):
                for f_ in pool.ops:
                    f_(e)

            @block.sync
            def _(e):
                for f_ in sp.ops:
                    f_(e)
    return nc


_CACHE = {}


def _run(nc_key, builder, in_maps):
    if nc_key not in _CACHE:
        _CACHE[nc_key] = builder()
    return run_bass_kernel_spmd(_CACHE[nc_key], in_maps, core_ids=list(range(8)))


def kernel(**inputs):
    shared = _host_shared(inputs)
    in_maps = []
    for b in range(8):
        m = dict(shared)
        m["xt"] = _host_xt(inputs, b)
        m["par"] = _host_params(inputs, b)
        in_maps.append(m)
    res = _run("fused", lambda: build((0, 1), True), in_maps)
    out = np.empty((8, S, D), np.float32)
    for b in range(8):
        o = res.results[b]["out"]
        out[b] = o.transpose(2, 1, 0).reshape(S, D)
    return out
```

```python
import numpy as np
from contextlib import ExitStack
import concourse.bass as bass
import concourse.mybir as mybir
from concourse.bass_utils import run_bass_kernel_spmd

F32 = mybir.dt.float32
BF16 = mybir.dt.bfloat16
AF = mybir.ActivationFunctionType
ALU = mybir.AluOpType

D = 1024
S = 2048
CTX = 256
T = S + CTX
DEPTH = 2
DFF = 2816
NJ = DFF // 128
NEG = -30000.0
POOL_WINDOWS = (2, 4, 8, 16)
TBS = [(0, 512), (512, 512), (1024, 512), (1536, 512), (2048, 256)]
GROUPS = [(0, 6), (6, 12), (12, 17), (17, 22)]

P_C2 = 0
P_LAYER = 16
P_LSZ = 72 + 24 + 6 + 2
P_FG = P_LAYER + 2 * P_LSZ
P_INVW = P_FG + 8
P_EC = P_INVW + 2
NPAR = P_EC + 32
C_ID = 0
C_ONES = 128
C_RM = 256
NCB = 256 + 4 * 256
RM_IDX = {0: 0, 1: 1, 4: 2, 5: 3}


def _host_shared(inp):
    f = lambda a: np.ascontiguousarray(a, dtype=np.float32)
    w_in1 = np.asarray(inp["ffn_w_in"], np.float32)
    W1 = f(w_in1.reshape(2, 2, 8, 128, 2, NJ, 128).transpose(0, 1, 5, 3, 2, 4, 6).reshape(2, 2, NJ, 128, 2048))
    W2 = f(np.asarray(inp["ffn_w_out"], np.float32).reshape(2, 2, NJ, 128, 1024))
    WIN = f(np.asarray(inp["w_in"], np.float32).reshape(2, 8, 128, 20, 128).transpose(0, 3, 2, 1, 4).reshape(2, 20, 128, 1024))
    WOUT = f(np.asarray(inp["w_out"], np.float32).reshape(2, 2, 4, 128, 1024).transpose(0, 1, 3, 2, 4).reshape(2, 2, 128, 4096))
    WMOD = f(np.asarray(inp["w_mod"], np.float32).reshape(2, 8, 128, 9, 4, 256).transpose(0, 3, 4, 2, 1, 5).reshape(2, 9, 4, 128, 2048))
    pw = np.asarray(inp["pool_w"], np.float32)
    PBD = np.zeros((2, 2, 128, 128), np.float32)
    for l in range(2):
        for cc in range(2):
            for g in range(2):
                PBD[l, cc, g * 64:(g + 1) * 64, g * 64:(g + 1) * 64] = pw[l, 2 * cc + g]
    rpb = np.asarray(inp["rpb"], np.float32)
    e = np.arange(2)[:, None, None, None]
    kc = np.arange(64)[None, :, None, None]
    s = np.arange(14)[None, None, :, None]
    qc = np.arange(64)[None, None, None, :]
    dr = np.broadcast_to(13 - s + e, (2, 64, 14, 64))
    dc = np.broadcast_to(np.clip(kc - qc + 15, 0, 30), (2, 64, 14, 64))
    cstart = np.clip(qc - 8, 0, 48)
    valid = np.broadcast_to((kc >= cstart) & (kc < cstart + 16), (2, 64, 14, 64))
    vals = rpb[:, :, dr, dc]
    WT_full = np.where(valid[None, None], vals, np.float32(NEG)).astype(np.float32).reshape(2, 8, 1, 128, 896)
    rowok = (dr >= 3) & (dr <= 10)
    WT_int = np.where((valid & rowok)[None, None], vals, np.float32(NEG)).astype(np.float32).reshape(2, 8, 1, 128, 896)
    WT = np.concatenate([WT_full, WT_int], axis=2)
    CB = np.zeros((128, NCB), np.float32)
    CB[:, C_ID:C_ID + 128] = np.eye(128, dtype=np.float32)
    CB[:, C_ONES:C_ONES + 128] = 1.0
    for cc, idx in RM_IDX.items():
        ee = np.arange(2)[:, None, None, None]
        ii = np.arange(4)[None, None, :, None]
        drr = np.broadcast_to(2 * cc + ee - ii + 3, (2, 64, 4, 64))
        CB[:, C_RM + idx * 256:C_RM + (idx + 1) * 256] = np.where((drr >= 3) & (drr <= 10), 0.0, NEG).reshape(128, 256)
    return dict(w1=W1, w2=W2, win=WIN, wout=WOUT, wmod=WMOD, pbd=PBD, wt=f(WT), cb=CB)


def _host_params(inp, b):
    P = np.zeros((128, NPAR), np.float32)
    cm = lambda v: np.asarray(v, np.float32).reshape(8, 128).T
    c2 = np.stack([cm(inp["c"][b]), cm(inp["c_ctx"])], axis=-1)
    P[:, P_C2:P_C2 + 16] = c2.reshape(128, 16)
    for l in range(2):
        o = P_LAYER + l * P_LSZ
        bm = np.asarray(inp["b_mod"], np.float32)[l].reshape(9, 8, 128).transpose(2, 0, 1)
        P[:, o:o + 72] = bm.reshape(128, 72)
        ng = np.asarray(inp["norm_g"], np.float32)[l].reshape(3, 8, 128).transpose(2, 0, 1)
        P[:, o + 72:o + 96] = ng.reshape(128, 24)
        cw = np.asarray(inp["conv_w"], np.float32)[l].reshape(3, 2, 128).transpose(2, 0, 1)
        P[:, o + 96:o + 102] = cw.reshape(128, 6)
        P[:, o + 102:o + 104] = np.asarray(inp["pool_scale"], np.float32)[l].reshape(2, 128).T
    P[:, P_FG:P_FG + 8] = cm(inp["final_g"])
    for cc in range(2):
        for half in range(2):
            w = POOL_WINDOWS[2 * cc + half]
            left = w // 2
            right = w - 1 - left
            rows = slice(half * 64, (half + 1) * 64)
            P[rows, P_INVW + cc] = 1.0 / w
            for t in range(8):
                cnt = t + right + 1 - max(t - left, 0)
                P[rows, P_EC + cc * 16 + t] = w / min(cnt, w)
                cnt2 = min(right + 1, 8 - t) + left
                P[rows, P_EC + cc * 16 + 8 + t] = w / min(cnt2, w)
    return P


def _host_xt(inp, b):
    xa = np.concatenate([np.asarray(inp["x"][b], np.float32), np.asarray(inp["ctx"][b], np.float32)], axis=0)
    return np.ascontiguousarray(xa.T.reshape(8, 128, T).transpose(1, 0, 2))


class Buf:
    __slots__ = ("w", "r", "sem", "semv", "name")

    def __init__(self, name=""):
        self.w = None
        self.r = {}
        self.sem = None
        self.semv = 0
        self.name = name


class Eng:
    def __init__(self, name, sem):
        self.name = name
        self.sem = sem
        self.n = 0
        self.ops = []
        self.waited = {}

    def wait(self, tok):
        if tok is None:
            return
        s, v = tok
        k = id(s)
        if self.waited.get(k, 0) >= v:
            return
        self.waited[k] = v
        self.ops.append(lambda e, s=s, v=v: e.wait_ge(s, v))


class Prog:
    def __init__(self, nc, es):
        self.nc = nc
        self.es = es
        mk = lambda n: Eng(n, es.enter_context(nc.semaphore("sem_" + n)))
        self.pe, self.act, self.dve, self.pool, self.sp = mk("pe"), mk("act"), mk("dve"), mk("pool"), mk("sp")
        self.nsem = 0
        self.phase_toks = []

    def _deps(self, eng, ins, outs):
        for b in ins:
            eng.wait(b.w)
        for b in outs:
            if b.r:
                for t in b.r.values():
                    eng.wait(t)
            elif b.w is not None and not (eng is self.pe and b.w[0] is eng.sem):
                eng.wait(b.w)

    def _mark(self, tok, ins, outs):
        for b in ins:
            b.r[id(tok[0])] = tok
        for b in outs:
            b.w = tok
            b.r = {}

    def op(self, eng, fn, ins=(), outs=()):
        self._deps(eng, ins, outs)
        eng.n += 1
        tok = (eng.sem, eng.n)
        eng.ops.append(lambda e, fn=fn, s=eng.sem: fn(e).then_inc(s, 1))
        self._mark(tok, ins, outs)
        return tok

    def dma(self, eng, pairs, ins=(), outs=(), dbuf=None, phase=False):
        if phase:
            for t in self.phase_toks:
                eng.wait(t)
        self._deps(eng, ins, outs)
        if dbuf.sem is None:
            dbuf.sem = self.es.enter_context(self.nc.semaphore("dsem%d" % self.nsem))
            self.nsem += 1
        for (o, i) in pairs:
            dbuf.semv += 16
            eng.ops.append(lambda e, o=o, i=i, s=dbuf.sem: e.dma_start(out=o, in_=i).then_inc(s, 16))
        tok = (dbuf.sem, dbuf.semv)
        self._mark(tok, ins, outs)
        return tok

    def barrier(self):
        engs = [self.pe, self.act, self.dve]
        toks = [(e.sem, e.n) for e in engs if e.n > 0]
        self.phase_toks = toks
        for e in engs:
            for t in toks:
                if t[0] is not e.sem:
                    e.wait(t)


def build(layer_list=(0, 1), do_final=True, stop=None, dbg=False):
    nc = bass.Bass("TRN2", target_bir_lowering=False)
    din = lambda name, shape: nc.dram_tensor(name, shape, F32, kind="ExternalInput").ap()
    XT = din("xt", [128, 8, T])
    PAR = din("par", [128, NPAR])
    CBD = din("cb", [128, NCB])
    W1 = din("w1", [2, 2, NJ, 128, 2048])
    W2 = din("w2", [2, 2, NJ, 128, 1024])
    WIN = din("win", [2, 20, 128, 1024])
    WOUT = din("wout", [2, 2, 128, 4096])
    WMOD = din("wmod", [2, 9, 4, 128, 2048])
    PBD = din("pbd", [2, 2, 128, 128])
    WT = din("wt", [2, 8, 2, 128, 896])
    OUT = nc.dram_tensor("out", [128, 8, S if do_final else T], F32, kind="ExternalOutput").ap()
    DBG = nc.dram_tensor("dbg", [128, 8, T], F32, kind="ExternalOutput").ap() if dbg else None

    with ExitStack() as es:
        sb = lambda name, shape, dt: es.enter_context(nc.sbuf_tensor(name, shape, dt))
        xT = sb("xT", [128, 8, T], F32)
        hT = sb("hT", [128, 8, T], BF16)
        NR = 35008
        R = sb("R", [128, NR], BF16)
        WS = sb("WS", [128, 6144], BF16)
        W2R = sb("W2R", [128, 6144], BF16)
        par = sb("par_sb", [128, NPAR], F32)
        cb = sb("cb_sb", [128, NCB], BF16)
        scT = sb("scT", [128, 8, 2], BF16)
        MODSL = [sb("MODS%d" % l_, [128, 9, 2, 8], F32) for l_ in range(2)]
        DERL = [sb("DER%d" % l_, [128, 3, 2, 3, 8], F32) for l_ in range(2)]
        pbd = sb("pbd_sb", [128, 2, 128], BF16)
        psb = [es.enter_context(nc.psum_tensor("ps%d" % i, [128, 512], F32)) for i in range(8)]
        P = Prog(nc, es)
        pe, act, dve, pool, sp = P.pe, P.act, P.dve, P.pool, P.sp

        def rb(off, n):
            return R[:, off:off + n]

        def rf(off, n):
            return R[:, off:off + 2 * n].bitcast(F32)

        NT0 = NR - 5120
        sq_t = [rb(NT0 + i * 512, 512) for i in range(2)]
        rt_t = rf(NT0 + 1024, 512)
        rstd_t = rf(NT0 + 2048, 512)
        tmp_t = [rf(NT0 + 3072 + i * 1024, 512) for i in range(2)]
        b_sq = [Buf("sq0"), Buf("sq1")]
        b_rt, b_rstd = Buf("rt"), Buf("rstd")
        b_tmp = [Buf("tmp0"), Buf("tmp1")]
        gT = rb(0, 6 * T).rearrange("p (j t) -> p j t", t=T)
        sa_t = [rf(13824 + i * 1024, 512) for i in range(2)]
        b_sa = [Buf("sa0"), Buf("sa1")]
        wm_t = [rb(15872 + i * 2048, 2048).rearrange("p (k c) -> p k c", c=256) for i in range(3)]
        b_wm = [Buf("wm0"), Buf("wm1"), Buf("wm2")]
        yTh = rb(0, 4 * T).rearrange("p (j t) -> p j t", t=T)
        S1o, S2o, S3o, S4o = 9216, 13888, 18560, 23232
        PTo, RDo, WTo = 25536, 27072, 28096
        NPT = 6
        PT = [rb(PTo + i * 512, 512) for i in range(3)] + [rb(S4o + i * 512, 512) for i in range(3)]
        b_PT = [Buf("PT%d" % i) for i in range(NPT)]
        rden = [rf(RDo + i * 512, 256) for i in range(2)]
        b_rden = [Buf("rd0"), Buf("rd1")]
        WTp = rb(WTo, 1792).rearrange("p (h c) -> p h c", c=896)
        b_WTp = Buf("WTp")

        b_x = [[Buf("x%d_%d" % (k, t)) for t in range(5)] for k in range(8)]
        b_h = [Buf("h%d" % t) for t in range(5)]
        b_g = [[Buf("g%d_%d" % (j, t)) for t in range(5)] for j in range(6)]
        b_y = [[Buf("y%d_%d" % (j, t)) for t in range(5)] for j in range(4)]
        b_ps = [Buf("ps%d" % i) for i in range(8)]
        b_ws = [Buf("ws%d" % i) for i in range(6)]
        b_w2 = Buf("w2")
        b_par, b_cb, b_sc, b_pbd = Buf("par"), Buf("cb"), Buf("sc"), Buf("pbd")
        b_modsl = [[Buf("mods%d_%d" % (l_, mi)) for mi in range(9)] for l_ in range(2)]
        b_derl = [[Buf("der%d_%d" % (l_, i_)) for i_ in range(3)] for l_ in range(2)]
        b_hgl = [[Buf("hg%d_%d" % (l_, i_)) for i_ in range(3)] for l_ in range(2)]
        cur = {"l": 0}
        b_out = Buf("out")
        ident = cb[:, C_ID:C_ID + 128]
        ones = cb[:, C_ONES:C_ONES + 128]

        st = {"s": 0, "l": 0}

        def ps_s():
            i = st["s"] % 6
            st["s"] += 1
            return psb[i], b_ps[i]

        def ps_l():
            i = 6 + st["l"] % 2
            st["l"] += 1
            return psb[i], b_ps[i]

        def ws_w1(s_):
            return WS[:, s_ * 2048:(s_ + 1) * 2048].rearrange("p (k c) -> p k c", c=256), [b_ws[2 * s_], b_ws[2 * s_ + 1]]

        def ws_win(s_):
            return WS[:, s_ * 1024:(s_ + 1) * 1024].rearrange("p (k c) -> p k c", c=128), [b_ws[s_]]

        def mm_group(out_ap, pairs, ins, outb, extra_out=()):
            n = len(pairs)

            def fn(e, out_ap=out_ap, pairs=pairs, n=n):
                r = None
                for i, (l_, r_) in enumerate(pairs):
                    r = e.matmul(out_ap, l_, r_, start=(i == 0), stop=(i == n - 1))
                return r
            return P.op(pe, fn, ins=ins, outs=[outb] + list(extra_out))

        if True:
            P.dma(sp, [(par[:], PAR)], outs=[b_par], dbuf=b_par)
            P.dma(pool, [(cb[:], CBD)], outs=[b_cb], dbuf=b_cb)
            def load_x(tb):
                t0, n = TBS[tb]
                bl = [b_x[k][tb] for k in range(8)]
                P.dma(sp, [(xT[:, :, t0:t0 + n], XT[:, :, t0:t0 + n])], outs=bl, dbuf=b_x[0][tb])
            load_x(0)
            P.op(act, lambda e: e.activation(out=scT[:].rearrange("p k w -> p (k w)"), in_=par[:, P_C2:P_C2 + 16], func=AF.Silu),
                 ins=[b_par], outs=[b_sc])

        def pcol(c):
            return par[:, c:c + 1]

        mod_items = []

        def make_mod_items(l):
            lo = P_LAYER + l * P_LSZ
            MODS, DER = MODSL[l], DERL[l]
            state = {}

            def item(mi, q):
                def run():
                    if q == 0:
                        state["ps"] = ps_l()
                    pst, bp = state["ps"]
                    s_ = st.get("wm", 0) % 3
                    st["wm"] = s_ + 1
                    P.dma(pool, [(wm_t[s_].rearrange("p k c -> p (k c)"), WMOD[l, mi, q])], outs=[b_wm[s_]], dbuf=b_wm[s_], phase=True)
                    for fl in range(2):
                        fc = q * 2 + fl
                        pairs = [(wm_t[s_][:, kc, fl * 128:(fl + 1) * 128], scT[:, kc, :]) for kc in range(8)]
                        mm_group(pst[:, fc * 2:fc * 2 + 2], pairs, ins=[b_wm[s_], b_sc], outb=bp)
                    if q == 3:
                        pv = pst[:, 0:16].rearrange("p (f w) -> p f w", w=2)
                        for wh in range(2):
                            P.op(dve, lambda e, wh=wh: e.tensor_tensor(
                                out=MODS[:, mi, wh, :], in0=pv[:, :, wh], in1=par[:, lo + mi * 8:lo + mi * 8 + 8], op=ALU.add),
                                ins=[bp, b_par], outs=[b_modsl[l][mi]])
                        i = mi // 3
                        if mi % 3 == 1:
                            for wh in range(2):
                                g_ap = par[:, lo + 72 + i * 8:lo + 72 + i * 8 + 8]
                                P.op(dve, lambda e, wh=wh, g_ap=g_ap: e.scalar_tensor_tensor(
                                    out=DER[:, i, wh, 0, :], in0=MODS[:, 3 * i + 1, wh, :], scalar=1.0, in1=g_ap, op0=ALU.add, op1=ALU.mult),
                                    ins=[b_modsl[l][3 * i + 1], b_par], outs=[b_derl[l][i]])
                                P.op(dve, lambda e, wh=wh: e.tensor_copy(out=DER[:, i, wh, 1, :], in_=MODS[:, 3 * i, wh, :]),
                                     ins=[b_modsl[l][3 * i]], outs=[b_derl[l][i]])
                        if mi % 3 == 2:
                            for wh in range(2):
                                hs = 1.0 if i == 1 else 0.5
                                P.op(dve, lambda e, wh=wh, hs=hs: e.tensor_scalar(
                                    out=DER[:, i, wh, 2, :], in0=MODS[:, 3 * i + 2, wh, :], scalar1=hs, scalar2=None, op0=ALU.mult),
                                    ins=[b_modsl[l][3 * i + 2]], outs=[b_hgl[l][i]])
                return run
            for mi in range(9):
                for q in range(4):
                    mod_items.append(item(mi, q))

        def pump_mods(n):
            for _ in range(n):
                if mod_items:
                    mod_items.pop(0)()

        def norm(tb, gs_fn, sh_fn, dst_fn, dst_bufs, dep):
            t0, n = TBS[tb]
            xin = [b_x[k][tb] for k in range(8)]
            pst, bp = ps_s()
            for kc in range(8):
                s_ = kc % 2
                P.op(act, lambda e, kc=kc, s_=s_: e.activation(out=sq_t[s_][:, :n], in_=xT[:, kc, t0:t0 + n], func=AF.Square),
                     ins=[b_x[kc][tb]], outs=[b_sq[s_]])

                def fn(e, kc=kc, s_=s_, pst=pst):
                    return e.matmul(pst[:, :n], ones, sq_t[s_][:, :n], start=(kc == 0), stop=(kc == 7))
                P.op(pe, fn, ins=[b_sq[s_], b_cb], outs=[bp])
            P.op(act, lambda e, pst=pst: e.activation(out=rt_t[:, :n], in_=pst[:, :n], func=AF.Sqrt, bias=1e-6, scale=1.0 / D),
                 ins=[bp], outs=[b_rt])
            P.op(dve, lambda e: e.reciprocal(out=rstd_t[:, :n], in_=rt_t[:, :n]), ins=[b_rt], outs=[b_rstd])
            for kc in range(8):
                s_ = kc % 2
                P.op(dve, lambda e, kc=kc, s_=s_: e.tensor_tensor(
                    out=tmp_t[s_][:, :n], in0=xT[:, kc, t0:t0 + n], in1=rstd_t[:, :n], op=ALU.mult),
                    ins=[b_x[kc][tb], b_rstd], outs=[b_tmp[s_]])
                if sh_fn is None:
                    P.op(act, lambda e, kc=kc, s_=s_: e.activation(
                        out=dst_fn(kc), in_=tmp_t[s_][:, :n], func=AF.Identity, scale=gs_fn(kc)),
                        ins=[b_tmp[s_]] + dep, outs=dst_bufs)
                else:
                    P.op(act, lambda e, kc=kc, s_=s_: e.activation(
                        out=dst_fn(kc), in_=tmp_t[s_][:, :n], func=AF.Identity, bias=sh_fn(kc), scale=gs_fn(kc)),
                        ins=[b_tmp[s_]] + dep, outs=dst_bufs)

        def mod_norm(tb, i, l):
            t0, n = TBS[tb]
            wh = 1 if tb == 4 else 0
            DER = DERL[l]
            norm(tb, lambda kc: DER[:, i, wh, 0, kc:kc + 1], lambda kc: DER[:, i, wh, 1, kc:kc + 1],
                 lambda kc: hT[:, kc, t0:t0 + n], [b_h[tb]], [b_derl[l][i]])

        def ffn(l, f, blocks, pre_normed=False, next_norm=None):
            i = 0 if f == 0 else 2
            P.barrier()
            if not pre_normed:
                for tb in blocks:
                    mod_norm(tb, i, l)
            for gi, (j0, j1) in enumerate(GROUPS):
                ng = j1 - j0
                for j in range(j0, j1):
                    wv, wb = ws_w1(j % 3)
                    P.dma(pool, [(wv.rearrange("p k c -> p (k c)"), W1[l, f, j])], outs=wb, dbuf=wb[0])
                    if j == j0 + 1 or ng == 1:
                        P.dma(pool, [(W2R[:, jl * 1024:(jl + 1) * 1024], W2[l, f, j0 + jl]) for jl in range(ng)],
                              outs=[b_w2], dbuf=b_w2)
                    if j >= 3:
                        pump_mods(2)
                    for tb in blocks:
                        t0, n = TBS[tb]
                        pa, ba = ps_s()
                        pb_, bb = ps_s()
                        mm_group(pa[:, :n], [(wv[:, kc, 0:128], hT[:, kc, t0:t0 + n]) for kc in range(8)], ins=wb + [b_h[tb]], outb=ba)
                        mm_group(pb_[:, :n], [(wv[:, kc, 128:256], hT[:, kc, t0:t0 + n]) for kc in range(8)], ins=wb + [b_h[tb]], outb=bb)
                        ss = st.get("sa", 0) % 2
                        st["sa"] = ss + 1
                        P.op(act, lambda e, pa=pa, ss=ss, n=n: e.activation(out=sa_t[ss][:, :n], in_=pa[:, :n], func=AF.Silu),
                             ins=[ba], outs=[b_sa[ss]])
                        P.op(dve, lambda e, pb_=pb_, ss=ss, n=n, j=j, j0=j0, t0=t0: e.tensor_tensor(
                            out=gT[:, j - j0, t0:t0 + n], in0=sa_t[ss][:, :n], in1=pb_[:, :n], op=ALU.mult),
                            ins=[b_sa[ss], bb], outs=[b_g[j - j0][tb]])
                hook = next_norm if gi == len(GROUPS) - 1 else None
                for bi, tb in enumerate(blocks):
                    t0, n = TBS[tb]
                    wh = 1 if tb == 4 else 0
                    if hook is not None and bi >= 1:
                        hook(blocks[bi - 1])
                    for oc in range(8):
                        po, bo = ps_s()
                        pairs = [(W2R[:, jl * 1024 + oc * 128:jl * 1024 + (oc + 1) * 128], gT[:, jl, t0:t0 + n]) for jl in range(ng)]
                        mm_group(po[:, :n], pairs, ins=[b_w2] + [b_g[jl][tb] for jl in range(ng)], outb=bo)
                        P.op(dve, lambda e, po=po, oc=oc, t0=t0, n=n, wh=wh: e.scalar_tensor_tensor(
                            out=xT[:, oc, t0:t0 + n], in0=po[:, :n], scalar=DERL[l][:, i, wh, 2, oc:oc + 1], in1=xT[:, oc, t0:t0 + n],
                            op0=ALU.mult, op1=ALU.add),
                            ins=[bo, b_hgl[l][i], b_x[oc][tb]], outs=[b_x[oc][tb]])
                if hook is not None:
                    hook(blocks[-1])

        def mixer(l, last, pre_normed=False, next_norm=None):
            lo = P_LAYER + l * P_LSZ
            xb = [0, 1, 2, 3] if last else [0, 1, 2, 3, 4]
            P.barrier()
            if not pre_normed:
                for tb in range(5):
                    mod_norm(tb, 1, l)
            P.dma(pool, [(pbd[:, cc, :], PBD[l, cc]) for cc in range(2)], outs=[b_pbd], dbuf=b_pbd)
            S1, S2, S3 = rf(S1o, 2336), rf(S2o, 2336), rf(S3o, 2336)
            Dt = rb(S4o, 2304)

            def proj_fm(slot, tb):
                t0, n = TBS[tb]
                wv, wb = ws_win(slot)
                pp, bpp = ps_s()
                mm_group(pp[:, :n], [(wv[:, kc, :], hT[:, kc, t0:t0 + n]) for kc in range(8)], ins=wb + [b_h[tb]], outb=bpp)
                return pp, bpp

            def load_win(slot, cch):
                wv, wb = ws_win(slot)
                P.dma(pool, [(wv.rearrange("p k c -> p (k c)"), WIN[l, cch])], outs=wb, dbuf=wb[0])


            b_S1 = [Buf("S1_%d" % t) for t in range(5)]
            b_S2 = [Buf("S2_%d" % t) for t in range(5)]
            b_S3 = [Buf("S3_%d" % t) for t in range(5)]
            b_D = [Buf("D_%d" % t) for t in range(5)]
            for (a_, b_) in ((0, 8), (2056, 2072), (2328, 2336)):
                P.op(dve, lambda e, a_=a_, b_=b_: e.memset(S1[:, a_:b_], 0.0), outs=b_S1)

            def col0(tb):
                return 8 + TBS[tb][0] + (16 if tb == 4 else 0)

            def nb(bl, tb):
                return [bl[t] for t in (tb - 1, tb, tb + 1) if 0 <= t < 5]

            def conv_evac(cc, sl, tb):
                t0, n = TBS[tb]
                a0 = col0(tb)
                ph, bh = proj_fm(sl[0], tb)
                pB, bB = proj_fm(sl[1], tb)
                pC, bC = proj_fm(sl[2], tb)
                ss = tb % 2
                P.op(act, lambda e: e.activation(out=tmp_t[ss][:, :n], in_=pC[:, :n], func=AF.Identity), ins=[bC], outs=[b_tmp[ss]])
                P.op(dve, lambda e: e.tensor_tensor(out=S1[:, a0:a0 + n], in0=ph[:, :n], in1=tmp_t[ss][:, :n], op=ALU.mult),
                     ins=[bh, b_tmp[ss]], outs=[b_S1[tb]])
                P.op(act, lambda e: e.activation(out=S2[:, a0:a0 + n], in_=pB[:, :n], func=AF.Identity), ins=[bB], outs=[b_S2[tb]])

            def conv_tail(cc, tb):
                t0, n = TBS[tb]
                a0 = col0(tb)
                w0, w1, w2 = [pcol(lo + 96 + k * 2 + cc) for k in range(3)]
                acc = S3[:, a0:a0 + n]
                P.op(dve, lambda e: e.tensor_scalar(out=acc, in0=S1[:, a0:a0 + n], scalar1=w1, scalar2=None, op0=ALU.mult),
                     ins=[b_S1[tb], b_par], outs=[b_S3[tb]])
                P.op(dve, lambda e: e.scalar_tensor_tensor(out=acc, in0=S1[:, a0 - 1:a0 - 1 + n], scalar=w0, in1=acc, op0=ALU.mult, op1=ALU.add),
                     ins=nb(b_S1, tb) + [b_S3[tb], b_par], outs=[b_S3[tb]])
                P.op(dve, lambda e: e.scalar_tensor_tensor(out=acc, in0=S1[:, a0 + 1:a0 + 1 + n], scalar=w2, in1=acc, op0=ALU.mult, op1=ALU.add),
                     ins=nb(b_S1, tb) + [b_S3[tb], b_par], outs=[b_S3[tb]])
                P.op(dve, lambda e: e.tensor_tensor(out=yTh[:, cc, t0:t0 + n], in0=acc, in1=S2[:, a0:a0 + n], op=ALU.mult),
                     ins=[b_S3[tb], b_S2[tb]], outs=[b_y[cc][tb]])

            def pool_evac(cc, sl, tb):
                t0, n = TBS[tb]
                a0 = col0(tb)
                pp, bpp = proj_fm(sl[0], tb)
                P.op(act, lambda e: e.activation(out=S1[:, a0:a0 + n], in_=pp[:, :n], func=AF.Identity), ins=[bpp], outs=[b_S1[tb]])

            def pool_tail(cc, tb):
                t0, n = TBS[tb]
                a0 = col0(tb)
                b0 = a0 + n

                def shadd(dst, bd, src, bs, lo_, hi_, sh):
                    P.op(dve, lambda e: e.tensor_tensor(out=dst[:, lo_:hi_], in0=src[:, lo_ + sh:hi_ + sh], in1=src[:, lo_ - sh:hi_ - sh], op=ALU.add),
                         ins=nb(bs, tb), outs=nb(bd, tb))
                P.op(dve, lambda e: e.tensor_tensor(out=S2[:, a0 - 7:b0 + 7], in0=S1[:, a0 - 7:b0 + 7], in1=S1[:, a0 - 8:b0 + 6], op=ALU.add),
                     ins=nb(b_S1, tb), outs=nb(b_S2, tb))
                shadd(S3, b_S3, S2, b_S2, a0 - 6, b0 + 6, 1)
                if cc == 1:
                    shadd(S2, b_S2, S3, b_S3, a0 - 4, b0 + 4, 2)
                    shadd(S3, b_S3, S2, b_S2, a0, b0, 4)
                for half, (Pb, bP) in enumerate(((S2, b_S2), (S3, b_S3))):
                    rows = slice(half * 64, (half + 1) * 64)
                    fixes = []
                    if tb in (0, 4):
                        fixes.append((a0, par[rows, P_EC + cc * 16:P_EC + cc * 16 + 8]))
                    if tb in (3, 4):
                        fixes.append((b0 - 8, par[rows, P_EC + cc * 16 + 8:P_EC + cc * 16 + 16]))
                    for (c_, ecap) in fixes:
                        P.op(dve, lambda e, c_=c_, ecap=ecap, Pb=Pb, rows=rows: e.tensor_tensor(
                            out=Pb[rows, c_:c_ + 8], in0=Pb[rows, c_:c_ + 8], in1=ecap, op=ALU.mult), ins=[bP[tb], b_par], outs=[bP[tb]])
                    P.op(dve, lambda e, Pb=Pb, rows=rows: e.scalar_tensor_tensor(
                        out=Dt[rows, t0:t0 + n], in0=Pb[rows, a0:b0], scalar=par[rows, P_INVW + cc:P_INVW + cc + 1], in1=S1[rows, a0:b0],
                        op0=ALU.mult, op1=ALU.subtract), ins=[bP[tb], b_S1[tb], b_par], outs=[b_D[tb]])
                pp, bpp = ps_s()
                mm_group(pp[:, :n], [(pbd[:, cc, :], Dt[:, t0:t0 + n])], ins=[b_pbd, b_D[tb]], outb=bpp)
                P.op(act, lambda e: e.activation(out=yTh[:, 2 + cc, t0:t0 + n], in_=pp[:, :n], func=AF.Identity, scale=pcol(lo + 102 + cc)),
                     ins=[bpp, b_par], outs=[b_y[2 + cc][tb]])

            units = [(pool_evac, pool_tail, 0, (0,), (6,)), (pool_evac, pool_tail, 1, (1,), (7,)),
                     (conv_evac, conv_tail, 0, (2, 3, 4), (0, 2, 4)), (conv_evac, conv_tail, 1, (5, 0, 1), (1, 3, 5))]
            for (_, _, _, sl, cchs) in units[:3]:
                for s_, c_ in zip(sl, cchs):
                    load_win(s_, c_)
            for u in range(len(units) + 1):
                if u == 2:
                    for s_, c_ in zip(units[3][3], units[3][4]):
                        load_win(s_, c_)
                prev = units[u - 1] if u >= 1 else None
                curu = units[u] if u < len(units) else None
                if prev is not None:
                    prev[1](prev[2], xb[0])
                for bi, tb in enumerate(xb):
                    if prev is not None and bi + 1 < len(xb):
                        prev[1](prev[2], xb[bi + 1])
                    if curu is not None:
                        curu[0](curu[2], curu[3], tb)

            def wout_pass(hf):
                P.dma(pool, [(W2R[:, 0:4096], WOUT[l, hf])], outs=[b_w2], dbuf=b_w2)
                hook = next_norm if hf == 1 else None
                for bi, tb in enumerate(xb):
                    t0, n = TBS[tb]
                    wh = 1 if tb == 4 else 0
                    if hook is not None and bi >= 1:
                        hook(xb[bi - 1])
                    for oc in range(8):
                        po, bo = ps_s()
                        pairs = [(W2R[:, c * 1024 + oc * 128:c * 1024 + (oc + 1) * 128], yTh[:, c, t0:t0 + n]) for c in range(4)]
                        mm_group(po[:, :n], pairs, ins=[b_w2] + [b_y[c][tb] for c in range(4)], outb=bo)
                        P.op(dve, lambda e, po=po, oc=oc, t0=t0, n=n, wh=wh: e.scalar_tensor_tensor(
                            out=xT[:, oc, t0:t0 + n], in0=po[:, :n], scalar=DERL[l][:, 1, wh, 2, oc:oc + 1], in1=xT[:, oc, t0:t0 + n],
                            op0=ALU.mult, op1=ALU.add), ins=[bo, b_hgl[l][1], b_x[oc][tb]], outs=[b_x[oc][tb]])
                if hook is not None:
                    hook(xb[-1])

            if dbg:
                P.dma(pool, [(DBG[:, 0:4, :], yTh[:, :, :])], ins=[b_y[c][t] for c in range(4) for t in range(5)], dbuf=b_out)
            wout_pass(0)

            QT0 = rb(S1o, T)
            KT = rb(S1o + T, T)
            Vp = rb(S2o, 18 * 192).rearrange("p (t c) -> p t c", c=192)
            QT1 = rb(S3o, T)
            WTi = rb(S3o + T, 1792).rearrange("p (h c) -> p h c", c=896)
            QTs = (QT0, QT1)
            b_Q = [Buf("Q%d" % t) for t in range(5)]
            b_K = [Buf("K%d" % t) for t in range(5)]
            b_V = [Buf("V%d" % t) for t in range(5)]
            b_const = Buf("qzero_vones")
            first_pair = True
            for p in range(4):
                sq_, sk_, sv_ = [(2 + 3 * p + i_) % 6 for i_ in range(3)]
                load_win(sq_, 8 + p)
                load_win(sk_, 12 + p)
                load_win(sv_, 16 + p)
                if first_pair:
                    P.barrier()
                    first_pair = False
                    P.op(dve, lambda e: e.memset(QT0[64:128, :], 0.0), outs=[b_const])
                    P.op(dve, lambda e: e.memset(QT1[0:64, :], 0.0), outs=[b_const])
                    P.op(dve, lambda e: e.memset(Vp[:, :, 64:128], 1.0), outs=[b_const])
                P.dma(pool, [(WTp[:, hh, :], WT[l, 2 * p + hh, 0]) for hh in range(2)] + [(WTi[:, hh, :], WT[l, 2 * p + hh, 1]) for hh in range(2)],
                      outs=[b_WTp], dbuf=b_WTp, phase=True)
                for tb in xb:
                    t0, n = TBS[tb]
                    pp, bpp = proj_fm(sq_, tb)
                    P.op(act, lambda e, pp=pp, t0=t0, n=n: e.activation(out=QT0[0:64, t0:t0 + n], in_=pp[0:64, :n], func=AF.Identity, scale=0.125),
                         ins=[bpp], outs=[b_Q[tb]])
                    P.op(act, lambda e, pp=pp, t0=t0, n=n: e.activation(out=QT1[64:128, t0:t0 + n], in_=pp[64:128, :n], func=AF.Identity, scale=0.125),
                         ins=[bpp], outs=[b_Q[tb]])
                for tb in range(5):
                    t0, n = TBS[tb]
                    pp, bpp = proj_fm(sk_, tb)
                    P.op(dve, lambda e, pp=pp, t0=t0, n=n: e.tensor_copy(out=KT[:, t0:t0 + n], in_=pp[:, :n]), ins=[bpp], outs=[b_K[tb]])
                wv, wb = ws_win(sv_)
                for tb in range(5):
                    t0, n = TBS[tb]
                    nt = n // 128
                    pp, bpp = ps_s()
                    for ti in range(nt):
                        tt = t0 // 128 + ti
                        mm_group(pp[:, ti * 128:(ti + 1) * 128],
                                 [(hT[:, kc, tt * 128:(tt + 1) * 128], wv[:, kc, :]) for kc in range(8)], ins=wb + [b_h[tb]], outb=bpp)
                    for hh in range(2):
                        P.op(dve, lambda e, pp=pp, t0=t0, nt=nt, n=n, hh=hh: e.tensor_copy(
                            out=Vp[:, t0 // 128:t0 // 128 + nt, hh * 128:hh * 128 + 64],
                            in_=pp[:, :n].rearrange("p (t c) -> p t c", c=128)[:, :, hh * 64:(hh + 1) * 64]),
                            ins=[bpp], outs=[b_V[tb]])

                qblocks = []
                for m in range(8):
                    if m == 0:
                        ccs = [2, 3, 4, 5]
                    elif m == 7:
                        ccs = [0, 1, 2, 3]
                    else:
                        ccs = [0, 1, 2, 3, 4, 5]
                    interior = (1 <= m <= 6)
                    chunks = [(2 * m - 2 + c, c, interior) for c in ccs] + [(16, None, False), (17, None, False)]
                    qblocks.append((256 * m, chunks))
                if not last:
                    qblocks.append((2048, [(16, None, False), (17, None, False)]))

                LAG = 3
                pending = []

                def flush(keep):
                    while len(pending) > keep:
                        pending.pop(0)()

                for (q0, chunks) in qblocks:
                    qtb = q0 // 512
                    po, bo = ps_l()
                    for hh in range(2):
                        npair = len(chunks) // 2
                        for cp in range(npair):
                            pS, bS = ps_s()
                            for ci in range(2):
                                kci, c, interior = chunks[2 * cp + ci]
                                ktb = (kci * 128) // 512
                                oap = pS[:, ci * 256:(ci + 1) * 256]
                                pairs = [(KT[:, kci * 128:(kci + 1) * 128], QTs[hh][:, q0:q0 + 256])]
                                ins_ = [b_K[ktb], b_Q[qtb], b_const]
                                if c is not None:
                                    Wsel = WTi if interior else WTp
                                    pairs.append((ident, Wsel[:, hh, (10 - 2 * c) * 64:(10 - 2 * c) * 64 + 256]))
                                    ins_ += [b_cb, b_WTp]
                                mm_group(oap, pairs, ins=ins_, outb=bS)
                            pi = st.get("pt", 0) % NPT
                            st["pt"] = pi + 1
                            P.op(act, lambda e, pS=pS, pi=pi: e.activation(out=PT[pi], in_=pS[:, :], func=AF.Exp),
                                 ins=[bS], outs=[b_PT[pi]])

                            def pv_job(cp=cp, pi=pi, hh=hh, chunks=chunks, po=po, bo=bo, npair=npair):
                                kcis = [chunks[2 * cp + ci][0] for ci in range(2)]
                                first_ = (hh == 0 and cp == 0)
                                last_ = (cp == npair - 1)

                                def fn(e):
                                    r = None
                                    for ci in range(2):
                                        r = e.matmul(po[:, hh * 256:(hh + 1) * 256], Vp[:, kcis[ci], hh * 64:hh * 64 + 128],
                                                     PT[pi][:, ci * 256:(ci + 1) * 256], start=(first_ and ci == 0), stop=(last_ and ci == 1),
                                                     skip_group_check=True)
                                    return r
                                P.op(pe, fn, ins=[b_V[(k * 128) // 512] for k in kcis] + [b_PT[pi], b_const], outs=[bo])
                            pending.append(pv_job)
                            flush(LAG)

                    def evac_job(po=po, bo=bo, q0=q0, p=p):
                        ri = st.get("rd", 0) % 2
                        st["rd"] = ri + 1
                        tb_ = q0 // 512
                        P.op(dve, lambda e: e.reciprocal(out=rden[ri][0:64, :], in_=po[64:128, 0:256]), ins=[bo], outs=[b_rden[ri]])
                        P.op(dve, lambda e: e.reciprocal(out=rden[ri][64:128, :], in_=po[0:64, 256:512]), ins=[bo], outs=[b_rden[ri]])
                        P.op(dve, lambda e: e.tensor_tensor(
                            out=yTh[0:64, p, q0:q0 + 256], in0=po[0:64, 0:256], in1=rden[ri][0:64, :], op=ALU.mult),
                            ins=[bo, b_rden[ri]], outs=[b_y[p][tb_]])
                        P.op(dve, lambda e: e.tensor_tensor(
                            out=yTh[64:128, p, q0:q0 + 256], in0=po[64:128, 256:512], in1=rden[ri][64:128, :], op=ALU.mult),
                            ins=[bo, b_rden[ri]], outs=[b_y[p][tb_]])
                    pending.append(evac_job)
                flush(0)
            if dbg:
                P.dma(pool, [(DBG[:, 4:8, :], yTh[:, :, :])], ins=[b_y[c][t] for c in range(4) for t in range(5)], dbuf=b_out)
            wout_pass(1)

        hflat = hT[:, :, :].rearrange("p k t -> p (k t)")
        ot = [hflat[:, i_ * 8192:(i_ + 1) * 8192].bitcast(F32).rearrange("p (k t) -> p k t", t=512) for i_ in range(2)]

        def final_norm(tb):
            t0, n = TBS[tb]
            oi = tb % 2
            norm(tb, lambda kc: par[:, P_FG + kc:P_FG + kc + 1], None, lambda kc: ot[oi][:, kc, :], list(b_h), [b_par])
            P.dma(sp, [(OUT[:, :, t0:t0 + n], ot[oi][:, :, :])], ins=list(b_h), dbuf=b_out)

        make_mod_items(layer_list[0])
        pump_mods(8)
        for s_ in range(3):
            if b_wm[s_].w is not None:
                sp.wait(b_wm[s_].w)
        for tb in range(1, 5):
            load_x(tb)
        nl = len(layer_list)
        stopped = False
        for li, l in enumerate(layer_list):
            last = (l == DEPTH - 1)
            cur["l"] = l
            full = stop is None
            ffn(l, 0, [0, 1, 2, 3, 4], pre_normed=(li > 0),
                next_norm=(lambda tb, l=l: mod_norm(tb, 1, l)) if stop != "ffn1" else None)
            pump_mods(100)
            if stop == "ffn1":
                stopped = True
                break
            blocks2 = [0, 1, 2, 3] if last else [0, 1, 2, 3, 4]
            mixer(l, last, pre_normed=True, next_norm=(lambda tb, l=l: mod_norm(tb, 2, l)) if stop != "mixer" else None)
            if stop == "mixer":
                stopped = True
                break
            if li + 1 < nl:
                make_mod_items(layer_list[li + 1])
                nn = (lambda tb, l2=layer_list[li + 1]: mod_norm(tb, 0, l2))
            elif do_final:
                nn = final_norm
            else:
                nn = None
            ffn(l, 1, blocks2, pre_normed=True, next_norm=nn)
            pump_mods(100)

        if not do_final:
            P.barrier()
            for tb, (t0, n) in enumerate(TBS):
                P.dma(sp, [(OUT[:, :, t0:t0 + n], xT[:, :, t0:t0 + n])], ins=[b_x[k][tb] for k in range(8)], dbuf=b_out)
        sp.wait((b_out.sem, b_out.semv))
        if dbg:
            pool.wait((b_out.sem, b_out.semv))

        with nc.Block() as block:
            @block.tensor
            def _(e):
                for f_ in pe.ops:
                    f_(e)

            @block.scalar
            def _(e):
                for f_ in act.ops:
                    f_(e)

            @block.vector
            def _(e):
                for f_ in dve.ops:
                    f_(e)

            @block.gpsimd
            def _(e):
                for f_ in pool.ops:
                    f_(e)

            @block.sync
            def _(e):
                for f_ in sp.ops:
                    f_(e)
    return nc


_CACHE = {}


def _run(nc_key, builder, in_maps):
    if nc_key not in _CACHE:
        _CACHE[nc_key] = builder()
    return run_bass_kernel_spmd(_CACHE[nc_key], in_maps, core_ids=list(range(8)))


def kernel(**inputs):
    shared = _host_shared(inputs)
    in_maps = []
    for b in range(8):
        m = dict(shared)
        m["xt"] = _host_xt(inputs, b)
        m["par"] = _host_params(inputs, b)
        in_maps.append(m)
    res = _run("fused", lambda: build((0, 1), True), in_maps)
    out = np.empty((8, S, D), np.float32)
    for b in range(8):
        o = res.results[b]["out"]
        out[b] = o.transpose(2, 1, 0).reshape(S, D)
    return out
```

```python
import numpy as np
from contextlib import ExitStack
import concourse.bass as bass
import concourse.mybir as mybir
from concourse.bass_utils import run_bass_kernel_spmd

F32 = mybir.dt.float32
BF16 = mybir.dt.bfloat16
AF = mybir.ActivationFunctionType
ALU = mybir.AluOpType

D = 1024
S = 2048
CTX = 256
T = S + CTX
DEPTH = 2
DFF = 2816
NJ = DFF // 128
NEG = -30000.0
POOL_WINDOWS = (2, 4, 8, 16)
TBS = [(0, 512), (512, 512), (1024, 512), (1536, 512), (2048, 256)]
GROUPS = [(0, 6), (6, 12), (12, 17), (17, 22)]

P_C2 = 0
P_LAYER = 16
P_LSZ = 72 + 24 + 6 + 2
P_FG = P_LAYER + 2 * P_LSZ
P_INVW = P_FG + 8
P_EC = P_INVW + 2
NPAR = P_EC + 32
C_ID = 0
C_ONES = 128
C_RM = 256
NCB = 256 + 4 * 256
RM_IDX = {0: 0, 1: 1, 4: 2, 5: 3}


def _host_shared(inp):
    f = lambda a: np.ascontiguousarray(a, dtype=np.float32)
    w_in1 = np.asarray(inp["ffn_w_in"], np.float32)
    W1 = f(w_in1.reshape(2, 2, 8, 128, 2, NJ, 128).transpose(0, 1, 5, 3, 2, 4, 6).reshape(2, 2, NJ, 128, 2048))
    W2 = f(np.asarray(inp["ffn_w_out"], np.float32).reshape(2, 2, NJ, 128, 1024))
    WIN = f(np.asarray(inp["w_in"], np.float32).reshape(2, 8, 128, 20, 128).transpose(0, 3, 2, 1, 4).reshape(2, 20, 128, 1024))
    WOUT = f(np.asarray(inp["w_out"], np.float32).reshape(2, 2, 4, 128, 1024).transpose(0, 1, 3, 2, 4).reshape(2, 2, 128, 4096))
    WMOD = f(np.asarray(inp["w_mod"], np.float32).reshape(2, 8, 128, 9, 4, 256).transpose(0, 3, 4, 2, 1, 5).reshape(2, 9, 4, 128, 2048))
    pw = np.asarray(inp["pool_w"], np.float32)
    PBD = np.zeros((2, 2, 128, 128), np.float32)
    for l in range(2):
        for cc in range(2):
            for g in range(2):
                PBD[l, cc, g * 64:(g + 1) * 64, g * 64:(g + 1) * 64] = pw[l, 2 * cc + g]
    rpb = np.asarray(inp["rpb"], np.float32)
    e = np.arange(2)[:, None, None, None]
    kc = np.arange(64)[None, :, None, None]
    s = np.arange(14)[None, None, :, None]
    qc = np.arange(64)[None, None, None, :]
    dr = np.broadcast_to(13 - s + e, (2, 64, 14, 64))
    dc = np.broadcast_to(np.clip(kc - qc + 15, 0, 30), (2, 64, 14, 64))
    cstart = np.clip(qc - 8, 0, 48)
    valid = np.broadcast_to((kc >= cstart) & (kc < cstart + 16), (2, 64, 14, 64))
    vals = rpb[:, :, dr, dc]
    WT_full = np.where(valid[None, None], vals, np.float32(NEG)).astype(np.float32).reshape(2, 8, 1, 128, 896)
    rowok = (dr >= 3) & (dr <= 10)
    WT_int = np.where((valid & rowok)[None, None], vals, np.float32(NEG)).astype(np.float32).reshape(2, 8, 1, 128, 896)
    WT = np.concatenate([WT_full, WT_int], axis=2)
    CB = np.zeros((128, NCB), np.float32)
    CB[:, C_ID:C_ID + 128] = np.eye(128, dtype=np.float32)
    CB[:, C_ONES:C_ONES + 128] = 1.0
    for cc, idx in RM_IDX.items():
        ee = np.arange(2)[:, None, None, None]
        ii = np.arange(4)[None, None, :, None]
        drr = np.broadcast_to(2 * cc + ee - ii + 3, (2, 64, 4, 64))
        CB[:, C_RM + idx * 256:C_RM + (idx + 1) * 256] = np.where((drr >= 3) & (drr <= 10), 0.0, NEG).reshape(128, 256)
    return dict(w1=W1, w2=W2, win=WIN, wout=WOUT, wmod=WMOD, pbd=PBD, wt=f(WT), cb=CB)


def _host_params(inp, b):
    P = np.zeros((128, NPAR), np.float32)
    cm = lambda v: np.asarray(v, np.float32).reshape(8, 128).T
    c2 = np.stack([cm(inp["c"][b]), cm(inp["c_ctx"])], axis=-1)
    P[:, P_C2:P_C2 + 16] = c2.reshape(128, 16)
    for l in range(2):
        o = P_LAYER + l * P_LSZ
        bm = np.asarray(inp["b_mod"], np.float32)[l].reshape(9, 8, 128).transpose(2, 0, 1)
        P[:, o:o + 72] = bm.reshape(128, 72)
        ng = np.asarray(inp["norm_g"], np.float32)[l].reshape(3, 8, 128).transpose(2, 0, 1)
        P[:, o + 72:o + 96] = ng.reshape(128, 24)
        cw = np.asarray(inp["conv_w"], np.float32)[l].reshape(3, 2, 128).transpose(2, 0, 1)
        P[:, o + 96:o + 102] = cw.reshape(128, 6)
        P[:, o + 102:o + 104] = np.asarray(inp["pool_scale"], np.float32)[l].reshape(2, 128).T
    P[:, P_FG:P_FG + 8] = cm(inp["final_g"])
    for cc in range(2):
        for half in range(2):
            w = POOL_WINDOWS[2 * cc + half]
            left = w // 2
            right = w - 1 - left
            rows = slice(half * 64, (half + 1) * 64)
            P[rows, P_INVW + cc] = 1.0 / w
            for t in range(8):
                cnt = t + right + 1 - max(t - left, 0)
                P[rows, P_EC + cc * 16 + t] = w / min(cnt, w)
                cnt2 = min(right + 1, 8 - t) + left
                P[rows, P_EC + cc * 16 + 8 + t] = w / min(cnt2, w)
    return P


def _host_xt(inp, b):
    xa = np.concatenate([np.asarray(inp["x"][b], np.float32), np.asarray(inp["ctx"][b], np.float32)], axis=0)
    return np.ascontiguousarray(xa.T.reshape(8, 128, T).transpose(1, 0, 2))


class Buf:
    __slots__ = ("w", "r", "sem", "semv", "name")

    def __init__(self, name=""):
        self.w = None
        self.r = {}
        self.sem = None
        self.semv = 0
        self.name = name


class Eng:
    def __init__(self, name, sem):
        self.name = name
        self.sem = sem
        self.n = 0
        self.ops = []
        self.waited = {}

    def wait(self, tok):
        if tok is None:
            return
        s, v = tok
        k = id(s)
        if self.waited.get(k, 0) >= v:
            return
        self.waited[k] = v
        self.ops.append(lambda e, s=s, v=v: e.wait_ge(s, v))


class Prog:
    def __init__(self, nc, es):
        self.nc = nc
        self.es = es
        mk = lambda n: Eng(n, es.enter_context(nc.semaphore("sem_" + n)))
        self.pe, self.act, self.dve, self.pool, self.sp = mk("pe"), mk("act"), mk("dve"), mk("pool"), mk("sp")
        self.nsem = 0
        self.phase_toks = []

    def _deps(self, eng, ins, outs):
        for b in ins:
            eng.wait(b.w)
        for b in outs:
            if b.r:
                for t in b.r.values():
                    eng.wait(t)
            elif b.w is not None and not (eng is self.pe and b.w[0] is eng.sem):
                eng.wait(b.w)

    def _mark(self, tok, ins, outs):
        for b in ins:
            b.r[id(tok[0])] = tok
        for b in outs:
            b.w = tok
            b.r = {}

    def op(self, eng, fn, ins=(), outs=()):
        self._deps(eng, ins, outs)
        eng.n += 1
        tok = (eng.sem, eng.n)
        eng.ops.append(lambda e, fn=fn, s=eng.sem: fn(e).then_inc(s, 1))
        self._mark(tok, ins, outs)
        return tok

    def dma(self, eng, pairs, ins=(), outs=(), dbuf=None, phase=False):
        if phase:
            for t in self.phase_toks:
                eng.wait(t)
        self._deps(eng, ins, outs)
        if dbuf.sem is None:
            dbuf.sem = self.es.enter_context(self.nc.semaphore("dsem%d" % self.nsem))
            self.nsem += 1
        for (o, i) in pairs:
            dbuf.semv += 16
            eng.ops.append(lambda e, o=o, i=i, s=dbuf.sem: e.dma_start(out=o, in_=i).then_inc(s, 16))
        tok = (dbuf.sem, dbuf.semv)
        self._mark(tok, ins, outs)
        return tok

    def barrier(self):
        engs = [self.pe, self.act, self.dve]
        toks = [(e.sem, e.n) for e in engs if e.n > 0]
        self.phase_toks = toks
        for e in (self.act, self.dve):
            for t in toks:
                if t[0] is not e.sem:
                    e.wait(t)


def build(layer_list=(0, 1), do_final=True, stop=None, dbg=False):
    nc = bass.Bass("TRN2", target_bir_lowering=False)
    din = lambda name, shape: nc.dram_tensor(name, shape, F32, kind="ExternalInput").ap()
    XT = din("xt", [128, 8, T])
    PAR = din("par", [128, NPAR])
    CBD = din("cb", [128, NCB])
    W1 = din("w1", [2, 2, NJ, 128, 2048])
    W2 = din("w2", [2, 2, NJ, 128, 1024])
    WIN = din("win", [2, 20, 128, 1024])
    WOUT = din("wout", [2, 2, 128, 4096])
    WMOD = din("wmod", [2, 9, 4, 128, 2048])
    PBD = din("pbd", [2, 2, 128, 128])
    WT = din("wt", [2, 8, 2, 128, 896])
    OUT = nc.dram_tensor("out", [128, 8, S if do_final else T], F32, kind="ExternalOutput").ap()
    DBG = nc.dram_tensor("dbg", [128, 8, T], F32, kind="ExternalOutput").ap() if dbg else None

    with ExitStack() as es:
        sb = lambda name, shape, dt: es.enter_context(nc.sbuf_tensor(name, shape, dt))
        xT = sb("xT", [128, 8, T], F32)
        hT = sb("hT", [128, 8, T], BF16)
        NR = 35008
        R = sb("R", [128, NR], BF16)
        WS = sb("WS", [128, 6144], BF16)
        W2R = sb("W2R", [128, 6144], BF16)
        par = sb("par_sb", [128, NPAR], F32)
        cb = sb("cb_sb", [128, NCB], BF16)
        scT = sb("scT", [128, 8, 2], BF16)
        MODSL = [sb("MODS%d" % l_, [128, 9, 2, 8], F32) for l_ in range(2)]
        DERL = [sb("DER%d" % l_, [128, 3, 2, 3, 8], F32) for l_ in range(2)]
        pbd = sb("pbd_sb", [128, 2, 128], BF16)
        psb = [es.enter_context(nc.psum_tensor("ps%d" % i, [128, 512], F32)) for i in range(8)]
        P = Prog(nc, es)
        pe, act, dve, pool, sp = P.pe, P.act, P.dve, P.pool, P.sp

        def rb(off, n):
            return R[:, off:off + n]

        def rf(off, n):
            return R[:, off:off + 2 * n].bitcast(F32)

        NT0 = NR - 5120
        sq_t = [rb(NT0 + i * 512, 512) for i in range(2)]
        rt_t = rf(NT0 + 1024, 512)
        rstd_t = rf(NT0 + 2048, 512)
        tmp_t = [rf(NT0 + 3072 + i * 1024, 512) for i in range(2)]
        b_sq = [Buf("sq0"), Buf("sq1")]
        b_rt, b_rstd = Buf("rt"), Buf("rstd")
        b_tmp = [Buf("tmp0"), Buf("tmp1")]
        gT = rb(0, 6 * T).rearrange("p (j t) -> p j t", t=T)
        sa_t = [rf(13824 + i * 1024, 512) for i in range(2)]
        b_sa = [Buf("sa0"), Buf("sa1")]
        wm_t = [rb(15872 + i * 2048, 2048).rearrange("p (k c) -> p k c", c=256) for i in range(3)]
        b_wm = [Buf("wm0"), Buf("wm1"), Buf("wm2")]
        yTh = rb(0, 4 * T).rearrange("p (j t) -> p j t", t=T)
        S1o, S2o, S3o, S4o = 9216, 13888, 18560, 23232
        PTo, RDo, WTo = 25536, 27072, 28096
        NPT = 6
        PT = [rb(PTo + i * 512, 512) for i in range(3)] + [rb(S4o + i * 512, 512) for i in range(3)]
        b_PT = [Buf("PT%d" % i) for i in range(NPT)]
        rden = [rf(RDo + i * 512, 256) for i in range(2)]
        b_rden = [Buf("rd0"), Buf("rd1")]
        WTp = rb(WTo, 1792).rearrange("p (h c) -> p h c", c=896)
        b_WTp = Buf("WTp")

        b_x = [[Buf("x%d_%d" % (k, t)) for t in range(5)] for k in range(8)]
        b_h = [Buf("h%d" % t) for t in range(5)]
        b_g = [[Buf("g%d_%d" % (j, t)) for t in range(5)] for j in range(6)]
        b_y = [[Buf("y%d_%d" % (j, t)) for t in range(5)] for j in range(4)]
        b_ps = [Buf("ps%d" % i) for i in range(8)]
        b_ws = [Buf("ws%d" % i) for i in range(6)]
        b_w2 = Buf("w2")
        b_par, b_cb, b_sc, b_pbd = Buf("par"), Buf("cb"), Buf("sc"), Buf("pbd")
        b_modsl = [[Buf("mods%d_%d" % (l_, mi)) for mi in range(9)] for l_ in range(2)]
        b_derl = [[Buf("der%d_%d" % (l_, i_)) for i_ in range(3)] for l_ in range(2)]
        b_hgl = [[Buf("hg%d_%d" % (l_, i_)) for i_ in range(3)] for l_ in range(2)]
        cur = {"l": 0}
        b_out = Buf("out")
        ident = cb[:, C_ID:C_ID + 128]
        ones = cb[:, C_ONES:C_ONES + 128]

        st = {"s": 0, "l": 0}

        def ps_s():
            i = st["s"] % 6
            st["s"] += 1
            return psb[i], b_ps[i]

        def ps_l():
            i = 6 + st["l"] % 2
            st["l"] += 1
            return psb[i], b_ps[i]

        def ws_w1(s_):
            return WS[:, s_ * 2048:(s_ + 1) * 2048].rearrange("p (k c) -> p k c", c=256), [b_ws[2 * s_], b_ws[2 * s_ + 1]]

        def ws_win(s_):
            return WS[:, s_ * 1024:(s_ + 1) * 1024].rearrange("p (k c) -> p k c", c=128), [b_ws[s_]]

        def mm_group(out_ap, pairs, ins, outb, extra_out=()):
            n = len(pairs)

            def fn(e, out_ap=out_ap, pairs=pairs, n=n):
                r = None
                for i, (l_, r_) in enumerate(pairs):
                    r = e.matmul(out_ap, l_, r_, start=(i == 0), stop=(i == n - 1))
                return r
            return P.op(pe, fn, ins=ins, outs=[outb] + list(extra_out))

        if True:
            P.dma(sp, [(par[:], PAR)], outs=[b_par], dbuf=b_par)
            P.dma(pool, [(cb[:], CBD)], outs=[b_cb], dbuf=b_cb)
            def load_x(tb):
                t0, n = TBS[tb]
                bl = [b_x[k][tb] for k in range(8)]
                P.dma(sp, [(xT[:, :, t0:t0 + n], XT[:, :, t0:t0 + n])], outs=bl, dbuf=b_x[0][tb])
            load_x(0)
            P.op(act, lambda e: e.activation(out=scT[:].rearrange("p k w -> p (k w)"), in_=par[:, P_C2:P_C2 + 16], func=AF.Silu),
                 ins=[b_par], outs=[b_sc])

        def pcol(c):
            return par[:, c:c + 1]

        mod_items = []

        def make_mod_items(l):
            lo = P_LAYER + l * P_LSZ
            MODS, DER = MODSL[l], DERL[l]
            state = {}

            def item(mi, q):
                def run():
                    if q == 0:
                        state["ps"] = ps_l()
                    pst, bp = state["ps"]
                    s_ = st.get("wm", 0) % 3
                    st["wm"] = s_ + 1
                    P.dma(pool, [(wm_t[s_].rearrange("p k c -> p (k c)"), WMOD[l, mi, q])], outs=[b_wm[s_]], dbuf=b_wm[s_], phase=True)
                    for fl in range(2):
                        fc = q * 2 + fl
                        pairs = [(wm_t[s_][:, kc, fl * 128:(fl + 1) * 128], scT[:, kc, :]) for kc in range(8)]
                        mm_group(pst[:, fc * 2:fc * 2 + 2], pairs, ins=[b_wm[s_], b_sc], outb=bp)
                    if q == 3:
                        pv = pst[:, 0:16].rearrange("p (f w) -> p f w", w=2)
                        for wh in range(2):
                            P.op(dve, lambda e, wh=wh: e.tensor_tensor(
                                out=MODS[:, mi, wh, :], in0=pv[:, :, wh], in1=par[:, lo + mi * 8:lo + mi * 8 + 8], op=ALU.add),
                                ins=[bp, b_par], outs=[b_modsl[l][mi]])
                        i = mi // 3
                        if mi % 3 == 1:
                            for wh in range(2):
                                g_ap = par[:, lo + 72 + i * 8:lo + 72 + i * 8 + 8]
                                P.op(dve, lambda e, wh=wh, g_ap=g_ap: e.scalar_tensor_tensor(
                                    out=DER[:, i, wh, 0, :], in0=MODS[:, 3 * i + 1, wh, :], scalar=1.0, in1=g_ap, op0=ALU.add, op1=ALU.mult),
                                    ins=[b_modsl[l][3 * i + 1], b_par], outs=[b_derl[l][i]])
                                P.op(dve, lambda e, wh=wh: e.tensor_copy(out=DER[:, i, wh, 1, :], in_=MODS[:, 3 * i, wh, :]),
                                     ins=[b_modsl[l][3 * i]], outs=[b_derl[l][i]])
                        if mi % 3 == 2:
                            for wh in range(2):
                                hs = 1.0 if i == 1 else 0.5
                                P.op(dve, lambda e, wh=wh, hs=hs: e.tensor_scalar(
                                    out=DER[:, i, wh, 2, :], in0=MODS[:, 3 * i + 2, wh, :], scalar1=hs, scalar2=None, op0=ALU.mult),
                                    ins=[b_modsl[l][3 * i + 2]], outs=[b_hgl[l][i]])
                return run
            for mi in range(9):
                for q in range(4):
                    mod_items.append(item(mi, q))

        def pump_mods(n):
            for _ in range(n):
                if mod_items:
                    mod_items.pop(0)()

        def norm(tb, gs_fn, sh_fn, dst_fn, dst_bufs, dep):
            t0, n = TBS[tb]
            xin = [b_x[k][tb] for k in range(8)]
            pst, bp = ps_s()
            for kc in range(8):
                s_ = kc % 2
                P.op(act, lambda e, kc=kc, s_=s_: e.activation(out=sq_t[s_][:, :n], in_=xT[:, kc, t0:t0 + n], func=AF.Square),
                     ins=[b_x[kc][tb]], outs=[b_sq[s_]])

                def fn(e, kc=kc, s_=s_, pst=pst):
                    return e.matmul(pst[:, :n], ones, sq_t[s_][:, :n], start=(kc == 0), stop=(kc == 7))
                P.op(pe, fn, ins=[b_sq[s_], b_cb], outs=[bp])
            P.op(act, lambda e, pst=pst: e.activation(out=rt_t[:, :n], in_=pst[:, :n], func=AF.Sqrt, bias=1e-6, scale=1.0 / D),
                 ins=[bp], outs=[b_rt])
            P.op(dve, lambda e: e.reciprocal(out=rstd_t[:, :n], in_=rt_t[:, :n]), ins=[b_rt], outs=[b_rstd])
            for kc in range(8):
                s_ = kc % 2
                P.op(dve, lambda e, kc=kc, s_=s_: e.tensor_tensor(
                    out=tmp_t[s_][:, :n], in0=xT[:, kc, t0:t0 + n], in1=rstd_t[:, :n], op=ALU.mult),
                    ins=[b_x[kc][tb], b_rstd], outs=[b_tmp[s_]])
                if sh_fn is None:
                    P.op(act, lambda e, kc=kc, s_=s_: e.activation(
                        out=dst_fn(kc), in_=tmp_t[s_][:, :n], func=AF.Identity, scale=gs_fn(kc)),
                        ins=[b_tmp[s_]] + dep, outs=dst_bufs)
                else:
                    P.op(act, lambda e, kc=kc, s_=s_: e.activation(
                        out=dst_fn(kc), in_=tmp_t[s_][:, :n], func=AF.Identity, bias=sh_fn(kc), scale=gs_fn(kc)),
                        ins=[b_tmp[s_]] + dep, outs=dst_bufs)

        def mod_norm(tb, i, l):
            t0, n = TBS[tb]
            wh = 1 if tb == 4 else 0
            DER = DERL[l]
            norm(tb, lambda kc: DER[:, i, wh, 0, kc:kc + 1], lambda kc: DER[:, i, wh, 1, kc:kc + 1],
                 lambda kc: hT[:, kc, t0:t0 + n], [b_h[tb]], [b_derl[l][i]])

        def ffn(l, f, blocks, pre_normed=False, next_norm=None):
            i = 0 if f == 0 else 2
            P.barrier()
            if not pre_normed:
                for tb in blocks:
                    mod_norm(tb, i, l)
            for gi, (j0, j1) in enumerate(GROUPS):
                ng = j1 - j0
                for j in range(j0, j1):
                    wv, wb = ws_w1(j % 3)
                    P.dma(pool, [(wv.rearrange("p k c -> p (k c)"), W1[l, f, j])], outs=wb, dbuf=wb[0])
                    if j == j0 + 1 or ng == 1:
                        P.dma(pool, [(W2R[:, jl * 1024:(jl + 1) * 1024], W2[l, f, j0 + jl]) for jl in range(ng)],
                              outs=[b_w2], dbuf=b_w2)
                    if j >= 3:
                        pump_mods(2)
                    for tb in blocks:
                        t0, n = TBS[tb]
                        pa, ba = ps_s()
                        pb_, bb = ps_s()
                        mm_group(pa[:, :n], [(wv[:, kc, 0:128], hT[:, kc, t0:t0 + n]) for kc in range(8)], ins=wb + [b_h[tb]], outb=ba)
                        mm_group(pb_[:, :n], [(wv[:, kc, 128:256], hT[:, kc, t0:t0 + n]) for kc in range(8)], ins=wb + [b_h[tb]], outb=bb)
                        ss = st.get("sa", 0) % 2
                        st["sa"] = ss + 1
                        P.op(act, lambda e, pa=pa, ss=ss, n=n: e.activation(out=sa_t[ss][:, :n], in_=pa[:, :n], func=AF.Silu),
                             ins=[ba], outs=[b_sa[ss]])
                        P.op(dve, lambda e, pb_=pb_, ss=ss, n=n, j=j, j0=j0, t0=t0: e.tensor_tensor(
                            out=gT[:, j - j0, t0:t0 + n], in0=sa_t[ss][:, :n], in1=pb_[:, :n], op=ALU.mult),
                            ins=[b_sa[ss], bb], outs=[b_g[j - j0][tb]])
                hook = next_norm if gi == len(GROUPS) - 1 else None
                for bi, tb in enumerate(blocks):
                    t0, n = TBS[tb]
                    wh = 1 if tb == 4 else 0
                    if hook is not None and bi >= 1:
                        hook(blocks[bi - 1])
                    for oc in range(8):
                        po, bo = ps_s()
                        pairs = [(W2R[:, jl * 1024 + oc * 128:jl * 1024 + (oc + 1) * 128], gT[:, jl, t0:t0 + n]) for jl in range(ng)]
                        mm_group(po[:, :n], pairs, ins=[b_w2] + [b_g[jl][tb] for jl in range(ng)], outb=bo)
                        P.op(dve, lambda e, po=po, oc=oc, t0=t0, n=n, wh=wh: e.scalar_tensor_tensor(
                            out=xT[:, oc, t0:t0 + n], in0=po[:, :n], scalar=DERL[l][:, i, wh, 2, oc:oc + 1], in1=xT[:, oc, t0:t0 + n],
                            op0=ALU.mult, op1=ALU.add),
                            ins=[bo, b_hgl[l][i], b_x[oc][tb]], outs=[b_x[oc][tb]])
                if hook is not None:
                    hook(blocks[-1])

        def mixer(l, last, pre_normed=False, next_norm=None):
            lo = P_LAYER + l * P_LSZ
            xb = [0, 1, 2, 3] if last else [0, 1, 2, 3, 4]
            P.barrier()
            if not pre_normed:
                for tb in range(5):
                    mod_norm(tb, 1, l)
            P.dma(pool, [(pbd[:, cc, :], PBD[l, cc]) for cc in range(2)], outs=[b_pbd], dbuf=b_pbd)
            S1, S2, S3 = rf(S1o, 2336), rf(S2o, 2336), rf(S3o, 2336)
            Dt = rb(S4o, 2304)

            def proj_fm(slot, tb):
                t0, n = TBS[tb]
                wv, wb = ws_win(slot)
                pp, bpp = ps_s()
                mm_group(pp[:, :n], [(wv[:, kc, :], hT[:, kc, t0:t0 + n]) for kc in range(8)], ins=wb + [b_h[tb]], outb=bpp)
                return pp, bpp

            def load_win(slot, cch):
                wv, wb = ws_win(slot)
                P.dma(pool, [(wv.rearrange("p k c -> p (k c)"), WIN[l, cch])], outs=wb, dbuf=wb[0])


            b_S1 = [Buf("S1_%d" % t) for t in range(5)]
            b_S2 = [Buf("S2_%d" % t) for t in range(5)]
            b_S3 = [Buf("S3_%d" % t) for t in range(5)]
            b_D = [Buf("D_%d" % t) for t in range(5)]
            for (a_, b_) in ((0, 8), (2056, 2072), (2328, 2336)):
                P.op(dve, lambda e, a_=a_, b_=b_: e.memset(S1[:, a_:b_], 0.0), outs=b_S1)

            def col0(tb):
                return 8 + TBS[tb][0] + (16 if tb == 4 else 0)

            def nb(bl, tb):
                return [bl[t] for t in (tb - 1, tb, tb + 1) if 0 <= t < 5]

            def conv_evac(cc, sl, tb):
                t0, n = TBS[tb]
                a0 = col0(tb)
                ph, bh = proj_fm(sl[0], tb)
                pB, bB = proj_fm(sl[1], tb)
                pC, bC = proj_fm(sl[2], tb)
                ss = tb % 2
                P.op(act, lambda e: e.activation(out=tmp_t[ss][:, :n], in_=pC[:, :n], func=AF.Identity), ins=[bC], outs=[b_tmp[ss]])
                P.op(dve, lambda e: e.tensor_tensor(out=S1[:, a0:a0 + n], in0=ph[:, :n], in1=tmp_t[ss][:, :n], op=ALU.mult),
                     ins=[bh, b_tmp[ss]], outs=[b_S1[tb]])
                P.op(act, lambda e: e.activation(out=S2[:, a0:a0 + n], in_=pB[:, :n], func=AF.Identity), ins=[bB], outs=[b_S2[tb]])

            def conv_tail(cc, tb):
                t0, n = TBS[tb]
                a0 = col0(tb)
                w0, w1, w2 = [pcol(lo + 96 + k * 2 + cc) for k in range(3)]
                acc = S3[:, a0:a0 + n]
                P.op(dve, lambda e: e.tensor_scalar(out=acc, in0=S1[:, a0:a0 + n], scalar1=w1, scalar2=None, op0=ALU.mult),
                     ins=[b_S1[tb], b_par], outs=[b_S3[tb]])
                P.op(dve, lambda e: e.scalar_tensor_tensor(out=acc, in0=S1[:, a0 - 1:a0 - 1 + n], scalar=w0, in1=acc, op0=ALU.mult, op1=ALU.add),
                     ins=nb(b_S1, tb) + [b_S3[tb], b_par], outs=[b_S3[tb]])
                P.op(dve, lambda e: e.scalar_tensor_tensor(out=acc, in0=S1[:, a0 + 1:a0 + 1 + n], scalar=w2, in1=acc, op0=ALU.mult, op1=ALU.add),
                     ins=nb(b_S1, tb) + [b_S3[tb], b_par], outs=[b_S3[tb]])
                P.op(dve, lambda e: e.tensor_tensor(out=yTh[:, cc, t0:t0 + n], in0=acc, in1=S2[:, a0:a0 + n], op=ALU.mult),
                     ins=[b_S3[tb], b_S2[tb]], outs=[b_y[cc][tb]])

            def pool_evac(cc, sl, tb):
                t0, n = TBS[tb]
                a0 = col0(tb)
                pp, bpp = proj_fm(sl[0], tb)
                P.op(act, lambda e: e.activation(out=S1[:, a0:a0 + n], in_=pp[:, :n], func=AF.Identity), ins=[bpp], outs=[b_S1[tb]])

            def pool_tail(cc, tb):
                t0, n = TBS[tb]
                a0 = col0(tb)
                b0 = a0 + n

                def shadd(dst, bd, src, bs, lo_, hi_, sh):
                    P.op(dve, lambda e: e.tensor_tensor(out=dst[:, lo_:hi_], in0=src[:, lo_ + sh:hi_ + sh], in1=src[:, lo_ - sh:hi_ - sh], op=ALU.add),
                         ins=nb(bs, tb), outs=nb(bd, tb))
                P.op(dve, lambda e: e.tensor_tensor(out=S2[:, a0 - 7:b0 + 7], in0=S1[:, a0 - 7:b0 + 7], in1=S1[:, a0 - 8:b0 + 6], op=ALU.add),
                     ins=nb(b_S1, tb), outs=nb(b_S2, tb))
                shadd(S3, b_S3, S2, b_S2, a0 - 6, b0 + 6, 1)
                if cc == 1:
                    shadd(S2, b_S2, S3, b_S3, a0 - 4, b0 + 4, 2)
                    shadd(S3, b_S3, S2, b_S2, a0, b0, 4)
                for half, (Pb, bP) in enumerate(((S2, b_S2), (S3, b_S3))):
                    rows = slice(half * 64, (half + 1) * 64)
                    fixes = []
                    if tb in (0, 4):
                        fixes.append((a0, par[rows, P_EC + cc * 16:P_EC + cc * 16 + 8]))
                    if tb in (3, 4):
                        fixes.append((b0 - 8, par[rows, P_EC + cc * 16 + 8:P_EC + cc * 16 + 16]))
                    for (c_, ecap) in fixes:
                        P.op(dve, lambda e, c_=c_, ecap=ecap, Pb=Pb, rows=rows: e.tensor_tensor(
                            out=Pb[rows, c_:c_ + 8], in0=Pb[rows, c_:c_ + 8], in1=ecap, op=ALU.mult), ins=[bP[tb], b_par], outs=[bP[tb]])
                    P.op(dve, lambda e, Pb=Pb, rows=rows: e.scalar_tensor_tensor(
                        out=Dt[rows, t0:t0 + n], in0=Pb[rows, a0:b0], scalar=par[rows, P_INVW + cc:P_INVW + cc + 1], in1=S1[rows, a0:b0],
                        op0=ALU.mult, op1=ALU.subtract), ins=[bP[tb], b_S1[tb], b_par], outs=[b_D[tb]])
                pp, bpp = ps_s()
                mm_group(pp[:, :n], [(pbd[:, cc, :], Dt[:, t0:t0 + n])], ins=[b_pbd, b_D[tb]], outb=bpp)
                P.op(act, lambda e: e.activation(out=yTh[:, 2 + cc, t0:t0 + n], in_=pp[:, :n], func=AF.Identity, scale=pcol(lo + 102 + cc)),
                     ins=[bpp, b_par], outs=[b_y[2 + cc][tb]])

            units = [(pool_evac, pool_tail, 0, (0,), (6,)), (pool_evac, pool_tail, 1, (1,), (7,)),
                     (conv_evac, conv_tail, 0, (2, 3, 4), (0, 2, 4)), (conv_evac, conv_tail, 1, (5, 0, 1), (1, 3, 5))]
            for (_, _, _, sl, cchs) in units[:3]:
                for s_, c_ in zip(sl, cchs):
                    load_win(s_, c_)
            for u in range(len(units) + 1):
                if u == 2:
                    for s_, c_ in zip(units[3][3], units[3][4]):
                        load_win(s_, c_)
                prev = units[u - 1] if u >= 1 else None
                curu = units[u] if u < len(units) else None
                if prev is not None:
                    prev[1](prev[2], xb[0])
                for bi, tb in enumerate(xb):
                    if prev is not None and bi + 1 < len(xb):
                        prev[1](prev[2], xb[bi + 1])
                    if curu is not None:
                        curu[0](curu[2], curu[3], tb)

            def wout_pass(hf):
                P.dma(pool, [(W2R[:, 0:4096], WOUT[l, hf])], outs=[b_w2], dbuf=b_w2)
                hook = next_norm if hf == 1 else None
                for bi, tb in enumerate(xb):
                    t0, n = TBS[tb]
                    wh = 1 if tb == 4 else 0
                    if hook is not None and bi >= 1:
                        hook(xb[bi - 1])
                    for oc in range(8):
                        po, bo = ps_s()
                        pairs = [(W2R[:, c * 1024 + oc * 128:c * 1024 + (oc + 1) * 128], yTh[:, c, t0:t0 + n]) for c in range(4)]
                        mm_group(po[:, :n], pairs, ins=[b_w2] + [b_y[c][tb] for c in range(4)], outb=bo)
                        P.op(dve, lambda e, po=po, oc=oc, t0=t0, n=n, wh=wh: e.scalar_tensor_tensor(
                            out=xT[:, oc, t0:t0 + n], in0=po[:, :n], scalar=DERL[l][:, 1, wh, 2, oc:oc + 1], in1=xT[:, oc, t0:t0 + n],
                            op0=ALU.mult, op1=ALU.add), ins=[bo, b_hgl[l][1], b_x[oc][tb]], outs=[b_x[oc][tb]])
                if hook is not None:
                    hook(xb[-1])

            if dbg:
                P.dma(pool, [(DBG[:, 0:4, :], yTh[:, :, :])], ins=[b_y[c][t] for c in range(4) for t in range(5)], dbuf=b_out)
            wout_pass(0)

            QT0 = rb(S1o, T)
            KT = rb(S1o + T, T)
            Vp = rb(S2o, 18 * 192).rearrange("p (t c) -> p t c", c=192)
            QT1 = rb(S3o, T)
            WTi = rb(S3o + T, 1792).rearrange("p (h c) -> p h c", c=896)
            QTs = (QT0, QT1)
            b_Q = [Buf("Q%d" % t) for t in range(5)]
            b_K = [Buf("K%d" % t) for t in range(5)]
            b_V = [Buf("V%d" % t) for t in range(5)]
            b_const = Buf("qzero_vones")
            first_pair = True
            for p in range(4):
                sq_, sk_, sv_ = [(2 + 3 * p + i_) % 6 for i_ in range(3)]
                load_win(sq_, 8 + p)
                load_win(sk_, 12 + p)
                load_win(sv_, 16 + p)
                if first_pair:
                    P.barrier()
                    first_pair = False
                    P.op(dve, lambda e: e.memset(QT0[64:128, :], 0.0), outs=[b_const])
                    P.op(dve, lambda e: e.memset(QT1[0:64, :], 0.0), outs=[b_const])
                    P.op(dve, lambda e: e.memset(Vp[:, :, 64:128], 1.0), outs=[b_const])
                P.dma(pool, [(WTp[:, hh, :], WT[l, 2 * p + hh, 0]) for hh in range(2)] + [(WTi[:, hh, :], WT[l, 2 * p + hh, 1]) for hh in range(2)],
                      outs=[b_WTp], dbuf=b_WTp, phase=True)
                for tb in xb:
                    t0, n = TBS[tb]
                    pp, bpp = proj_fm(sq_, tb)
                    P.op(act, lambda e, pp=pp, t0=t0, n=n: e.activation(out=QT0[0:64, t0:t0 + n], in_=pp[0:64, :n], func=AF.Identity, scale=0.125),
                         ins=[bpp], outs=[b_Q[tb]])
                    P.op(act, lambda e, pp=pp, t0=t0, n=n: e.activation(out=QT1[64:128, t0:t0 + n], in_=pp[64:128, :n], func=AF.Identity, scale=0.125),
                         ins=[bpp], outs=[b_Q[tb]])
                for tb in range(5):
                    t0, n = TBS[tb]
                    pp, bpp = proj_fm(sk_, tb)
                    P.op(dve, lambda e, pp=pp, t0=t0, n=n: e.tensor_copy(out=KT[:, t0:t0 + n], in_=pp[:, :n]), ins=[bpp], outs=[b_K[tb]])
                wv, wb = ws_win(sv_)
                for tb in range(5):
                    t0, n = TBS[tb]
                    nt = n // 128
                    pp, bpp = ps_s()
                    for ti in range(nt):
                        tt = t0 // 128 + ti
                        mm_group(pp[:, ti * 128:(ti + 1) * 128],
                                 [(hT[:, kc, tt * 128:(tt + 1) * 128], wv[:, kc, :]) for kc in range(8)], ins=wb + [b_h[tb]], outb=bpp)
                    for hh in range(2):
                        P.op(dve, lambda e, pp=pp, t0=t0, nt=nt, n=n, hh=hh: e.tensor_copy(
                            out=Vp[:, t0 // 128:t0 // 128 + nt, hh * 128:hh * 128 + 64],
                            in_=pp[:, :n].rearrange("p (t c) -> p t c", c=128)[:, :, hh * 64:(hh + 1) * 64]),
                            ins=[bpp], outs=[b_V[tb]])

                qblocks = []
                for m in range(8):
                    if m == 0:
                        ccs = [2, 3, 4, 5]
                    elif m == 7:
                        ccs = [0, 1, 2, 3]
                    else:
                        ccs = [0, 1, 2, 3, 4, 5]
                    interior = (1 <= m <= 6)
                    chunks = [(2 * m - 2 + c, c, interior) for c in ccs] + [(16, None, False), (17, None, False)]
                    qblocks.append((256 * m, chunks))
                if not last:
                    qblocks.append((2048, [(16, None, False), (17, None, False)]))

                LAG = 3
                pending = []

                def flush(keep):
                    while len(pending) > keep:
                        pending.pop(0)()

                for (q0, chunks) in qblocks:
                    qtb = q0 // 512
                    po, bo = ps_l()
                    for hh in range(2):
                        npair = len(chunks) // 2
                        for cp in range(npair):
                            pS, bS = ps_s()
                            for ci in range(2):
                                kci, c, interior = chunks[2 * cp + ci]
                                ktb = (kci * 128) // 512
                                oap = pS[:, ci * 256:(ci + 1) * 256]
                                pairs = [(KT[:, kci * 128:(kci + 1) * 128], QTs[hh][:, q0:q0 + 256])]
                                ins_ = [b_K[ktb], b_Q[qtb], b_const]
                                if c is not None:
                                    Wsel = WTi if interior else WTp
                                    pairs.append((ident, Wsel[:, hh, (10 - 2 * c) * 64:(10 - 2 * c) * 64 + 256]))
                                    ins_ += [b_cb, b_WTp]
                                mm_group(oap, pairs, ins=ins_, outb=bS)
                            pi = st.get("pt", 0) % NPT
                            st["pt"] = pi + 1
                            P.op(act, lambda e, pS=pS, pi=pi: e.activation(out=PT[pi], in_=pS[:, :], func=AF.Exp),
                                 ins=[bS], outs=[b_PT[pi]])

                            def pv_job(cp=cp, pi=pi, hh=hh, chunks=chunks, po=po, bo=bo, npair=npair):
                                kcis = [chunks[2 * cp + ci][0] for ci in range(2)]
                                first_ = (hh == 0 and cp == 0)
                                last_ = (cp == npair - 1)

                                def fn(e):
                                    r = None
                                    for ci in range(2):
                                        r = e.matmul(po[:, hh * 256:(hh + 1) * 256], Vp[:, kcis[ci], hh * 64:hh * 64 + 128],
                                                     PT[pi][:, ci * 256:(ci + 1) * 256], start=(first_ and ci == 0), stop=(last_ and ci == 1),
                                                     skip_group_check=True)
                                    return r
                                P.op(pe, fn, ins=[b_V[(k * 128) // 512] for k in kcis] + [b_PT[pi], b_const], outs=[bo])
                            pending.append(pv_job)
                            flush(LAG)

                    def evac_job(po=po, bo=bo, q0=q0, p=p):
                        ri = st.get("rd", 0) % 2
                        st["rd"] = ri + 1
                        tb_ = q0 // 512
                        P.op(dve, lambda e: e.reciprocal(out=rden[ri][0:64, :], in_=po[64:128, 0:256]), ins=[bo], outs=[b_rden[ri]])
                        P.op(dve, lambda e: e.reciprocal(out=rden[ri][64:128, :], in_=po[0:64, 256:512]), ins=[bo], outs=[b_rden[ri]])
                        P.op(dve, lambda e: e.tensor_tensor(
                            out=yTh[0:64, p, q0:q0 + 256], in0=po[0:64, 0:256], in1=rden[ri][0:64, :], op=ALU.mult),
                            ins=[bo, b_rden[ri]], outs=[b_y[p][tb_]])
                        P.op(dve, lambda e: e.tensor_tensor(
                            out=yTh[64:128, p, q0:q0 + 256], in0=po[64:128, 256:512], in1=rden[ri][64:128, :], op=ALU.mult),
                            ins=[bo, b_rden[ri]], outs=[b_y[p][tb_]])
                    pending.append(evac_job)
                flush(0)
            if dbg:
                P.dma(pool, [(DBG[:, 4:8, :], yTh[:, :, :])], ins=[b_y[c][t] for c in range(4) for t in range(5)], dbuf=b_out)
            wout_pass(1)

        hflat = hT[:, :, :].rearrange("p k t -> p (k t)")
        ot = [hflat[:, i_ * 8192:(i_ + 1) * 8192].bitcast(F32).rearrange("p (k t) -> p k t", t=512) for i_ in range(2)]

        def final_norm(tb):
            t0, n = TBS[tb]
            oi = tb % 2
            norm(tb, lambda kc: par[:, P_FG + kc:P_FG + kc + 1], None, lambda kc: ot[oi][:, kc, :], list(b_h), [b_par])
            P.dma(sp, [(OUT[:, :, t0:t0 + n], ot[oi][:, :, :])], ins=list(b_h), dbuf=b_out)

        make_mod_items(layer_list[0])
        pump_mods(8)
        for s_ in range(3):
            if b_wm[s_].w is not None:
                sp.wait(b_wm[s_].w)
        for tb in range(1, 5):
            load_x(tb)
        nl = len(layer_list)
        stopped = False
        for li, l in enumerate(layer_list):
            last = (l == DEPTH - 1)
            cur["l"] = l
            full = stop is None
            ffn(l, 0, [0, 1, 2, 3, 4], pre_normed=(li > 0),
                next_norm=(lambda tb, l=l: mod_norm(tb, 1, l)) if stop != "ffn1" else None)
            pump_mods(100)
            if stop == "ffn1":
                stopped = True
                break
            blocks2 = [0, 1, 2, 3] if last else [0, 1, 2, 3, 4]
            mixer(l, last, pre_normed=True, next_norm=(lambda tb, l=l: mod_norm(tb, 2, l)) if stop != "mixer" else None)
            if stop == "mixer":
                stopped = True
                break
            if li + 1 < nl:
                make_mod_items(layer_list[li + 1])
                nn = (lambda tb, l2=layer_list[li + 1]: mod_norm(tb, 0, l2))
            elif do_final:
                nn = final_norm
            else:
                nn = None
            ffn(l, 1, blocks2, pre_normed=True, next_norm=nn)
            pump_mods(100)

        if not do_final:
            P.barrier()
            for tb, (t0, n) in enumerate(TBS):
                P.dma(sp, [(OUT[:, :, t0:t0 + n], xT[:, :, t0:t0 + n])], ins=[b_x[k][tb] for k in range(8)], dbuf=b_out)
        sp.wait((b_out.sem, b_out.semv))
        if dbg:
            pool.wait((b_out.sem, b_out.semv))

        with nc.Block() as block:
            @block.tensor
            def _(e):
                for f_ in pe.ops:
                    f_(e)

            @block.scalar
            def _(e):
                for f_ in act.ops:
                    f_(e)

            @block.vector
            def _(e):
                for f_ in dve.ops:
                    f_(e)

            @block.gpsimd
            def _(e):
                for f_ in pool.ops:
                    f_(e)

            @block.sync
            def _(e):
                for f_ in sp.ops:
                    f_(e)
    return nc


_CACHE = {}


def _run(nc_key, builder, in_maps):
    if nc_key not in _CACHE:
        _CACHE[nc_key] = builder()
    return run_bass_kernel_spmd(_CACHE[nc_key], in_maps, core_ids=list(range(8)))


def kernel(**inputs):
    shared = _host_shared(inputs)
    in_maps = []
    for b in range(8):
        m = dict(shared)
        m["xt"] = _host_xt(inputs, b)
        m["par"] = _host_params(inputs, b)
        in_maps.append(m)
    res = _run("fused", lambda: build((0, 1), True), in_maps)
    out = np.empty((8, S, D), np.float32)
    for b in range(8):
        o = res.results[b]["out"]
        out[b] = o.transpose(2, 1, 0).reshape(S, D)
    return out
```

```python
import numpy as np
from contextlib import ExitStack
import concourse.bass as bass
import concourse.mybir as mybir
from concourse.bass_utils import run_bass_kernel_spmd

F32 = mybir.dt.float32
BF16 = mybir.dt.bfloat16
AF = mybir.ActivationFunctionType
ALU = mybir.AluOpType

D = 1024
S = 2048
CTX = 256
T = S + CTX
DEPTH = 2
DFF = 2816
NJ = DFF // 128
NEG = -30000.0
POOL_WINDOWS = (2, 4, 8, 16)
TBS = [(0, 512), (512, 512), (1024, 512), (1536, 512), (2048, 256)]
GROUPS = [(0, 6), (6, 12), (12, 17), (17, 22)]

P_C2 = 0
P_LAYER = 16
P_LSZ = 72 + 24 + 6 + 2
P_FG = P_LAYER + 2 * P_LSZ
P_INVW = P_FG + 8
P_EC = P_INVW + 2
NPAR = P_EC + 32
C_ID = 0
C_ONES = 128
C_RM = 256
NCB = 256 + 4 * 256
RM_IDX = {0: 0, 1: 1, 4: 2, 5: 3}


def _host_shared(inp):
    f = lambda a: np.ascontiguousarray(a, dtype=np.float32)
    w_in1 = np.asarray(inp["ffn_w_in"], np.float32)
    W1 = f(w_in1.reshape(2, 2, 8, 128, 2, NJ, 128).transpose(0, 1, 5, 3, 2, 4, 6).reshape(2, 2, NJ, 128, 2048))
    W2 = f(np.asarray(inp["ffn_w_out"], np.float32).reshape(2, 2, NJ, 128, 1024))
    WIN = f(np.asarray(inp["w_in"], np.float32).reshape(2, 8, 128, 20, 128).transpose(0, 3, 2, 1, 4).reshape(2, 20, 128, 1024))
    WOUT = f(np.asarray(inp["w_out"], np.float32).reshape(2, 2, 4, 128, 1024).transpose(0, 1, 3, 2, 4).reshape(2, 2, 128, 4096))
    WMOD = f(np.asarray(inp["w_mod"], np.float32).reshape(2, 8, 128, 9, 4, 256).transpose(0, 3, 4, 2, 1, 5).reshape(2, 9, 4, 128, 2048))
    pw = np.asarray(inp["pool_w"], np.float32)
    PBD = np.zeros((2, 2, 128, 128), np.float32)
    for l in range(2):
        for cc in range(2):
            for g in range(2):
                PBD[l, cc, g * 64:(g + 1) * 64, g * 64:(g + 1) * 64] = pw[l, 2 * cc + g]
    rpb = np.asarray(inp["rpb"], np.float32)
    e = np.arange(2)[:, None, None, None]
    kc = np.arange(64)[None, :, None, None]
    s = np.arange(14)[None, None, :, None]
    qc = np.arange(64)[None, None, None, :]
    dr = np.broadcast_to(13 - s + e, (2, 64, 14, 64))
    dc = np.broadcast_to(np.clip(kc - qc + 15, 0, 30), (2, 64, 14, 64))
    cstart = np.clip(qc - 8, 0, 48)
    valid = np.broadcast_to((kc >= cstart) & (kc < cstart + 16), (2, 64, 14, 64))
    vals = rpb[:, :, dr, dc]
    WT_full = np.where(valid[None, None], vals, np.float32(NEG)).astype(np.float32).reshape(2, 8, 1, 128, 896)
    rowok = (dr >= 3) & (dr <= 10)
    WT_int = np.where((valid & rowok)[None, None], vals, np.float32(NEG)).astype(np.float32).reshape(2, 8, 1, 128, 896)
    WT = np.concatenate([WT_full, WT_int], axis=2)
    CB = np.zeros((128, NCB), np.float32)
    CB[:, C_ID:C_ID + 128] = np.eye(128, dtype=np.float32)
    CB[:, C_ONES:C_ONES + 128] = 1.0
    for cc, idx in RM_IDX.items():
        ee = np.arange(2)[:, None, None, None]
        ii = np.arange(4)[None, None, :, None]
        drr = np.broadcast_to(2 * cc + ee - ii + 3, (2, 64, 4, 64))
        CB[:, C_RM + idx * 256:C_RM + (idx + 1) * 256] = np.where((drr >= 3) & (drr <= 10), 0.0, NEG).reshape(128, 256)
    return dict(w1=W1, w2=W2, win=WIN, wout=WOUT, wmod=WMOD, pbd=PBD, wt=f(WT), cb=CB)


def _host_params(inp, b):
    P = np.zeros((128, NPAR), np.float32)
    cm = lambda v: np.asarray(v, np.float32).reshape(8, 128).T
    c2 = np.stack([cm(inp["c"][b]), cm(inp["c_ctx"])], axis=-1)
    P[:, P_C2:P_C2 + 16] = c2.reshape(128, 16)
    for l in range(2):
        o = P_LAYER + l * P_LSZ
        bm = np.asarray(inp["b_mod"], np.float32)[l].reshape(9, 8, 128).transpose(2, 0, 1)
        P[:, o:o + 72] = bm.reshape(128, 72)
        ng = np.asarray(inp["norm_g"], np.float32)[l].reshape(3, 8, 128).transpose(2, 0, 1)
        P[:, o + 72:o + 96] = ng.reshape(128, 24)
        cw = np.asarray(inp["conv_w"], np.float32)[l].reshape(3, 2, 128).transpose(2, 0, 1)
        P[:, o + 96:o + 102] = cw.reshape(128, 6)
        P[:, o + 102:o + 104] = np.asarray(inp["pool_scale"], np.float32)[l].reshape(2, 128).T
    P[:, P_FG:P_FG + 8] = cm(inp["final_g"])
    for cc in range(2):
        for half in range(2):
            w = POOL_WINDOWS[2 * cc + half]
            left = w // 2
            right = w - 1 - left
            rows = slice(half * 64, (half + 1) * 64)
            P[rows, P_INVW + cc] = 1.0 / w
            for t in range(8):
                cnt = t + right + 1 - max(t - left, 0)
                P[rows, P_EC + cc * 16 + t] = w / min(cnt, w)
                cnt2 = min(right + 1, 8 - t) + left
                P[rows, P_EC + cc * 16 + 8 + t] = w / min(cnt2, w)
    return P


def _host_xt(inp, b):
    xa = np.concatenate([np.asarray(inp["x"][b], np.float32), np.asarray(inp["ctx"][b], np.float32)], axis=0)
    return np.ascontiguousarray(xa.T.reshape(8, 128, T).transpose(1, 0, 2))


class Buf:
    __slots__ = ("w", "r", "sem", "semv", "name")

    def __init__(self, name=""):
        self.w = None
        self.r = {}
        self.sem = None
        self.semv = 0
        self.name = name


class Eng:
    def __init__(self, name, sem):
        self.name = name
        self.sem = sem
        self.n = 0
        self.ops = []
        self.waited = {}

    def wait(self, tok):
        if tok is None:
            return
        s, v = tok
        k = id(s)
        if self.waited.get(k, 0) >= v:
            return
        self.waited[k] = v
        self.ops.append(lambda e, s=s, v=v: e.wait_ge(s, v))


class Prog:
    def __init__(self, nc, es):
        self.nc = nc
        self.es = es
        mk = lambda n: Eng(n, es.enter_context(nc.semaphore("sem_" + n)))
        self.pe, self.act, self.dve, self.pool, self.sp = mk("pe"), mk("act"), mk("dve"), mk("pool"), mk("sp")
        self.nsem = 0
        self.phase_toks = []

    def _deps(self, eng, ins, outs):
        for b in ins:
            eng.wait(b.w)
        for b in outs:
            if b.r:
                for t in b.r.values():
                    eng.wait(t)
            elif b.w is not None and not (eng is self.pe and b.w[0] is eng.sem):
                eng.wait(b.w)

    def _mark(self, tok, ins, outs):
        for b in ins:
            b.r[id(tok[0])] = tok
        for b in outs:
            b.w = tok
            b.r = {}

    def op(self, eng, fn, ins=(), outs=()):
        self._deps(eng, ins, outs)
        eng.n += 1
        tok = (eng.sem, eng.n)
        eng.ops.append(lambda e, fn=fn, s=eng.sem: fn(e).then_inc(s, 1))
        self._mark(tok, ins, outs)
        return tok

    def dma(self, eng, pairs, ins=(), outs=(), dbuf=None, phase=False):
        if phase:
            for t in self.phase_toks:
                eng.wait(t)
        self._deps(eng, ins, outs)
        if dbuf.sem is None:
            dbuf.sem = self.es.enter_context(self.nc.semaphore("dsem%d" % self.nsem))
            self.nsem += 1
        for (o, i) in pairs:
            dbuf.semv += 16
            eng.ops.append(lambda e, o=o, i=i, s=dbuf.sem: e.dma_start(out=o, in_=i).then_inc(s, 16))
        tok = (dbuf.sem, dbuf.semv)
        self._mark(tok, ins, outs)
        return tok

    def barrier(self):
        engs = [self.pe, self.act, self.dve]
        toks = [(e.sem, e.n) for e in engs if e.n > 0]
        self.phase_toks = toks
        for e in (self.act, self.dve):
            for t in toks:
                if t[0] is not e.sem:
                    e.wait(t)


def build(layer_list=(0, 1), do_final=True, stop=None, dbg=False):
    nc = bass.Bass("TRN2", target_bir_lowering=False)
    din = lambda name, shape: nc.dram_tensor(name, shape, F32, kind="ExternalInput").ap()
    XT = din("xt", [128, 8, T])
    PAR = din("par", [128, NPAR])
    CBD = din("cb", [128, NCB])
    W1 = din("w1", [2, 2, NJ, 128, 2048])
    W2 = din("w2", [2, 2, NJ, 128, 1024])
    WIN = din("win", [2, 20, 128, 1024])
    WOUT = din("wout", [2, 2, 128, 4096])
    WMOD = din("wmod", [2, 9, 4, 128, 2048])
    PBD = din("pbd", [2, 2, 128, 128])
    WT = din("wt", [2, 8, 2, 128, 896])
    OUT = nc.dram_tensor("out", [128, 8, S if do_final else T], F32, kind="ExternalOutput").ap()
    DBG = nc.dram_tensor("dbg", [128, 8, T], F32, kind="ExternalOutput").ap() if dbg else None

    with ExitStack() as es:
        sb = lambda name, shape, dt: es.enter_context(nc.sbuf_tensor(name, shape, dt))
        xT = sb("xT", [128, 8, T], F32)
        hT = sb("hT", [128, 8, T], BF16)
        NR = 35008
        R = sb("R", [128, NR], BF16)
        WS = sb("WS", [128, 6144], BF16)
        W2R = sb("W2R", [128, 6144], BF16)
        par = sb("par_sb", [128, NPAR], F32)
        cb = sb("cb_sb", [128, NCB], BF16)
        scT = sb("scT", [128, 8, 2], BF16)
        MODSL = [sb("MODS%d" % l_, [128, 9, 2, 8], F32) for l_ in range(2)]
        DERL = [sb("DER%d" % l_, [128, 3, 2, 3, 8], F32) for l_ in range(2)]
        pbd = sb("pbd_sb", [128, 2, 128], BF16)
        psb = [es.enter_context(nc.psum_tensor("ps%d" % i, [128, 512], F32)) for i in range(8)]
        P = Prog(nc, es)
        pe, act, dve, pool, sp = P.pe, P.act, P.dve, P.pool, P.sp

        def rb(off, n):
            return R[:, off:off + n]

        def rf(off, n):
            return R[:, off:off + 2 * n].bitcast(F32)

        NT0 = NR - 5120
        sq_t = [rb(NT0 + i * 512, 512) for i in range(2)]
        rt_t = rf(NT0 + 1024, 512)
        rstd_t = rf(NT0 + 2048, 512)
        tmp_t = [rf(NT0 + 3072 + i * 1024, 512) for i in range(2)]
        b_sq = [Buf("sq0"), Buf("sq1")]
        b_rt, b_rstd = Buf("rt"), Buf("rstd")
        b_tmp = [Buf("tmp0"), Buf("tmp1")]
        gT = rb(0, 6 * T).rearrange("p (j t) -> p j t", t=T)
        sa_t = [rf(13824 + i * 1024, 512) for i in range(2)]
        b_sa = [Buf("sa0"), Buf("sa1")]
        wm_t = [rb(15872 + i * 2048, 2048).rearrange("p (k c) -> p k c", c=256) for i in range(3)]
        b_wm = [Buf("wm0"), Buf("wm1"), Buf("wm2")]
        yTh = rb(0, 4 * T).rearrange("p (j t) -> p j t", t=T)
        S1o, S2o, S3o, S4o = 9216, 13888, 18560, 23232
        PTo, RDo, WTo = 25536, 27072, 28096
        NPT = 6
        PT = [rb(PTo + i * 512, 512) for i in range(3)] + [rb(S4o + i * 512, 512) for i in range(3)]
        b_PT = [Buf("PT%d" % i) for i in range(NPT)]
        rden = [rf(RDo + i * 512, 256) for i in range(2)]
        b_rden = [Buf("rd0"), Buf("rd1")]
        WTp = rb(WTo, 1792).rearrange("p (h c) -> p h c", c=896)
        b_WTp = Buf("WTp")

        b_x = [[Buf("x%d_%d" % (k, t)) for t in range(5)] for k in range(8)]
        b_h = [Buf("h%d" % t) for t in range(5)]
        b_g = [[Buf("g%d_%d" % (j, t)) for t in range(5)] for j in range(6)]
        b_y = [[Buf("y%d_%d" % (j, t)) for t in range(5)] for j in range(4)]
        b_ps = [Buf("ps%d" % i) for i in range(8)]
        b_ws = [Buf("ws%d" % i) for i in range(6)]
        b_w2 = Buf("w2")
        b_par, b_cb, b_sc, b_pbd = Buf("par"), Buf("cb"), Buf("sc"), Buf("pbd")
        b_modsl = [[Buf("mods%d_%d" % (l_, mi)) for mi in range(9)] for l_ in range(2)]
        b_derl = [[Buf("der%d_%d" % (l_, i_)) for i_ in range(3)] for l_ in range(2)]
        b_hgl = [[Buf("hg%d_%d" % (l_, i_)) for i_ in range(3)] for l_ in range(2)]
        cur = {"l": 0}
        b_out = Buf("out")
        ident = cb[:, C_ID:C_ID + 128]
        ones = cb[:, C_ONES:C_ONES + 128]

        st = {"s": 0, "l": 0}

        def ps_s():
            i = st["s"] % 6
            st["s"] += 1
            return psb[i], b_ps[i]

        def ps_l():
            i = 6 + st["l"] % 2
            st["l"] += 1
            return psb[i], b_ps[i]

        def ws_w1(s_):
            return WS[:, s_ * 2048:(s_ + 1) * 2048].rearrange("p (k c) -> p k c", c=256), [b_ws[2 * s_], b_ws[2 * s_ + 1]]

        def ws_win(s_):
            return WS[:, s_ * 1024:(s_ + 1) * 1024].rearrange("p (k c) -> p k c", c=128), [b_ws[s_]]

        def mm_group(out_ap, pairs, ins, outb, extra_out=()):
            n = len(pairs)

            def fn(e, out_ap=out_ap, pairs=pairs, n=n):
                r = None
                for i, (l_, r_) in enumerate(pairs):
                    r = e.matmul(out_ap, l_, r_, start=(i == 0), stop=(i == n - 1))
                return r
            return P.op(pe, fn, ins=ins, outs=[outb] + list(extra_out))

        if True:
            P.dma(sp, [(par[:], PAR)], outs=[b_par], dbuf=b_par)
            P.dma(pool, [(cb[:], CBD)], outs=[b_cb], dbuf=b_cb)
            def load_x(tb):
                t0, n = TBS[tb]
                bl = [b_x[k][tb] for k in range(8)]
                P.dma(sp, [(xT[:, :, t0:t0 + n], XT[:, :, t0:t0 + n])], outs=bl, dbuf=b_x[0][tb])
            load_x(0)
            P.op(act, lambda e: e.activation(out=scT[:].rearrange("p k w -> p (k w)"), in_=par[:, P_C2:P_C2 + 16], func=AF.Silu),
                 ins=[b_par], outs=[b_sc])

        def pcol(c):
            return par[:, c:c + 1]

        mod_items = []

        def make_mod_items(l):
            lo = P_LAYER + l * P_LSZ
            MODS, DER = MODSL[l], DERL[l]
            state = {}

            def item(mi, q):
                def run():
                    if q == 0:
                        state["ps"] = ps_l()
                    pst, bp = state["ps"]
                    s_ = st.get("wm", 0) % 3
                    st["wm"] = s_ + 1
                    P.dma(pool, [(wm_t[s_].rearrange("p k c -> p (k c)"), WMOD[l, mi, q])], outs=[b_wm[s_]], dbuf=b_wm[s_], phase=True)
                    for fl in range(2):
                        fc = q * 2 + fl
                        pairs = [(wm_t[s_][:, kc, fl * 128:(fl + 1) * 128], scT[:, kc, :]) for kc in range(8)]
                        mm_group(pst[:, fc * 2:fc * 2 + 2], pairs, ins=[b_wm[s_], b_sc], outb=bp)
                    if q == 3:
                        pv = pst[:, 0:16].rearrange("p (f w) -> p f w", w=2)
                        for wh in range(2):
                            P.op(dve, lambda e, wh=wh: e.tensor_tensor(
                                out=MODS[:, mi, wh, :], in0=pv[:, :, wh], in1=par[:, lo + mi * 8:lo + mi * 8 + 8], op=ALU.add),
                                ins=[bp, b_par], outs=[b_modsl[l][mi]])
                        i = mi // 3
                        if mi % 3 == 1:
                            for wh in range(2):
                                g_ap = par[:, lo + 72 + i * 8:lo + 72 + i * 8 + 8]
                                P.op(dve, lambda e, wh=wh, g_ap=g_ap: e.scalar_tensor_tensor(
                                    out=DER[:, i, wh, 0, :], in0=MODS[:, 3 * i + 1, wh, :], scalar=1.0, in1=g_ap, op0=ALU.add, op1=ALU.mult),
                                    ins=[b_modsl[l][3 * i + 1], b_par], outs=[b_derl[l][i]])
                                P.op(dve, lambda e, wh=wh: e.tensor_copy(out=DER[:, i, wh, 1, :], in_=MODS[:, 3 * i, wh, :]),
                                     ins=[b_modsl[l][3 * i]], outs=[b_derl[l][i]])
                        if mi % 3 == 2:
                            for wh in range(2):
                                hs = 1.0 if i == 1 else 0.5
                                P.op(dve, lambda e, wh=wh, hs=hs: e.tensor_scalar(
                                    out=DER[:, i, wh, 2, :], in0=MODS[:, 3 * i + 2, wh, :], scalar1=hs, scalar2=None, op0=ALU.mult),
                                    ins=[b_modsl[l][3 * i + 2]], outs=[b_hgl[l][i]])
                return run
            for mi in range(9):
                for q in range(4):
                    mod_items.append(item(mi, q))

        def pump_mods(n):
            for _ in range(n):
                if mod_items:
                    mod_items.pop(0)()

        def norm(tb, gs_fn, sh_fn, dst_fn, dst_bufs, dep):
            t0, n = TBS[tb]
            xin = [b_x[k][tb] for k in range(8)]
            pst, bp = ps_s()
            for kc in range(8):
                s_ = kc % 2
                P.op(act, lambda e, kc=kc, s_=s_: e.activation(out=sq_t[s_][:, :n], in_=xT[:, kc, t0:t0 + n], func=AF.Square),
                     ins=[b_x[kc][tb]], outs=[b_sq[s_]])

                def fn(e, kc=kc, s_=s_, pst=pst):
                    return e.matmul(pst[:, :n], ones, sq_t[s_][:, :n], start=(kc == 0), stop=(kc == 7))
                P.op(pe, fn, ins=[b_sq[s_], b_cb], outs=[bp])
            P.op(act, lambda e, pst=pst: e.activation(out=rt_t[:, :n], in_=pst[:, :n], func=AF.Sqrt, bias=1e-6, scale=1.0 / D),
                 ins=[bp], outs=[b_rt])
            P.op(dve, lambda e: e.reciprocal(out=rstd_t[:, :n], in_=rt_t[:, :n]), ins=[b_rt], outs=[b_rstd])
            for kc in range(8):
                s_ = kc % 2
                P.op(dve, lambda e, kc=kc, s_=s_: e.tensor_tensor(
                    out=tmp_t[s_][:, :n], in0=xT[:, kc, t0:t0 + n], in1=rstd_t[:, :n], op=ALU.mult),
                    ins=[b_x[kc][tb], b_rstd], outs=[b_tmp[s_]])
                if sh_fn is None:
                    P.op(act, lambda e, kc=kc, s_=s_: e.activation(
                        out=dst_fn(kc), in_=tmp_t[s_][:, :n], func=AF.Identity, scale=gs_fn(kc)),
                        ins=[b_tmp[s_]] + dep, outs=dst_bufs)
                else:
                    P.op(act, lambda e, kc=kc, s_=s_: e.activation(
                        out=dst_fn(kc), in_=tmp_t[s_][:, :n], func=AF.Identity, bias=sh_fn(kc), scale=gs_fn(kc)),
                        ins=[b_tmp[s_]] + dep, outs=dst_bufs)

        def mod_norm(tb, i, l):
            t0, n = TBS[tb]
            wh = 1 if tb == 4 else 0
            DER = DERL[l]
            norm(tb, lambda kc: DER[:, i, wh, 0, kc:kc + 1], lambda kc: DER[:, i, wh, 1, kc:kc + 1],
                 lambda kc: hT[:, kc, t0:t0 + n], [b_h[tb]], [b_derl[l][i]])

        def ffn(l, f, blocks, pre_normed=False, next_norm=None):
            i = 0 if f == 0 else 2
            P.barrier()
            if not pre_normed:
                for tb in blocks:
                    mod_norm(tb, i, l)
            for gi, (j0, j1) in enumerate(GROUPS):
                ng = j1 - j0
                for j in range(j0, j1):
                    wv, wb = ws_w1(j % 3)
                    P.dma(pool, [(wv.rearrange("p k c -> p (k c)"), W1[l, f, j])], outs=wb, dbuf=wb[0])
                    if j == j0 + 1 or ng == 1:
                        P.dma(pool, [(W2R[:, jl * 1024:(jl + 1) * 1024], W2[l, f, j0 + jl]) for jl in range(ng)],
                              outs=[b_w2], dbuf=b_w2)
                    if j >= 3:
                        pump_mods(2)
                    for tb in blocks:
                        t0, n = TBS[tb]
                        pa, ba = ps_s()
                        pb_, bb = ps_s()
                        mm_group(pa[:, :n], [(wv[:, kc, 0:128], hT[:, kc, t0:t0 + n]) for kc in range(8)], ins=wb + [b_h[tb]], outb=ba)
                        mm_group(pb_[:, :n], [(wv[:, kc, 128:256], hT[:, kc, t0:t0 + n]) for kc in range(8)], ins=wb + [b_h[tb]], outb=bb)
                        ss = st.get("sa", 0) % 2
                        st["sa"] = ss + 1
                        P.op(act, lambda e, pa=pa, ss=ss, n=n: e.activation(out=sa_t[ss][:, :n], in_=pa[:, :n], func=AF.Silu),
                             ins=[ba], outs=[b_sa[ss]])
                        P.op(dve, lambda e, pb_=pb_, ss=ss, n=n, j=j, j0=j0, t0=t0: e.tensor_tensor(
                            out=gT[:, j - j0, t0:t0 + n], in0=sa_t[ss][:, :n], in1=pb_[:, :n], op=ALU.mult),
                            ins=[b_sa[ss], bb], outs=[b_g[j - j0][tb]])
                hook = next_norm if gi == len(GROUPS) - 1 else None
                for bi, tb in enumerate(blocks):
                    t0, n = TBS[tb]
                    wh = 1 if tb == 4 else 0
                    for oc in range(8):
                        if hook is not None and bi >= 1 and oc == 4:
                            hook(blocks[bi - 1])
                        po, bo = ps_s()
                        pairs = [(W2R[:, jl * 1024 + oc * 128:jl * 1024 + (oc + 1) * 128], gT[:, jl, t0:t0 + n]) for jl in range(ng)]
                        mm_group(po[:, :n], pairs, ins=[b_w2] + [b_g[jl][tb] for jl in range(ng)], outb=bo)
                        P.op(dve, lambda e, po=po, oc=oc, t0=t0, n=n, wh=wh: e.scalar_tensor_tensor(
                            out=xT[:, oc, t0:t0 + n], in0=po[:, :n], scalar=DERL[l][:, i, wh, 2, oc:oc + 1], in1=xT[:, oc, t0:t0 + n],
                            op0=ALU.mult, op1=ALU.add),
                            ins=[bo, b_hgl[l][i], b_x[oc][tb]], outs=[b_x[oc][tb]])
                if hook is not None:
                    hook(blocks[-1])

        def mixer(l, last, pre_normed=False, next_norm=None):
            lo = P_LAYER + l * P_LSZ
            xb = [0, 1, 2, 3] if last else [0, 1, 2, 3, 4]
            P.barrier()
            if not pre_normed:
                for tb in range(5):
                    mod_norm(tb, 1, l)
            P.dma(pool, [(pbd[:, cc, :], PBD[l, cc]) for cc in range(2)], outs=[b_pbd], dbuf=b_pbd)
            S1, S2, S3 = rf(S1o, 2336), rf(S2o, 2336), rf(S3o, 2336)
            Dt = rb(S4o, 2304)

            def proj_fm(slot, tb):
                t0, n = TBS[tb]
                wv, wb = ws_win(slot)
                pp, bpp = ps_s()
                mm_group(pp[:, :n], [(wv[:, kc, :], hT[:, kc, t0:t0 + n]) for kc in range(8)], ins=wb + [b_h[tb]], outb=bpp)
                return pp, bpp

            def load_win(slot, cch):
                wv, wb = ws_win(slot)
                P.dma(pool, [(wv.rearrange("p k c -> p (k c)"), WIN[l, cch])], outs=wb, dbuf=wb[0])


            b_S1 = [Buf("S1_%d" % t) for t in range(5)]
            b_S2 = [Buf("S2_%d" % t) for t in range(5)]
            b_S3 = [Buf("S3_%d" % t) for t in range(5)]
            b_D = [Buf("D_%d" % t) for t in range(5)]
            for (a_, b_) in ((0, 8), (2056, 2072), (2328, 2336)):
                P.op(dve, lambda e, a_=a_, b_=b_: e.memset(S1[:, a_:b_], 0.0), outs=b_S1)

            def col0(tb):
                return 8 + TBS[tb][0] + (16 if tb == 4 else 0)

            def nb(bl, tb):
                return [bl[t] for t in (tb - 1, tb, tb + 1) if 0 <= t < 5]

            def conv_evac(cc, sl, tb):
                t0, n = TBS[tb]
                a0 = col0(tb)
                ph, bh = proj_fm(sl[0], tb)
                pB, bB = proj_fm(sl[1], tb)
                pC, bC = proj_fm(sl[2], tb)
                ss = tb % 2
                P.op(act, lambda e: e.activation(out=tmp_t[ss][:, :n], in_=pC[:, :n], func=AF.Identity), ins=[bC], outs=[b_tmp[ss]])
                P.op(dve, lambda e: e.tensor_tensor(out=S1[:, a0:a0 + n], in0=ph[:, :n], in1=tmp_t[ss][:, :n], op=ALU.mult),
                     ins=[bh, b_tmp[ss]], outs=[b_S1[tb]])
                P.op(act, lambda e: e.activation(out=S2[:, a0:a0 + n], in_=pB[:, :n], func=AF.Identity), ins=[bB], outs=[b_S2[tb]])

            def conv_tail(cc, tb):
                t0, n = TBS[tb]
                a0 = col0(tb)
                w0, w1, w2 = [pcol(lo + 96 + k * 2 + cc) for k in range(3)]
                acc = S3[:, a0:a0 + n]
                P.op(dve, lambda e: e.tensor_scalar(out=acc, in0=S1[:, a0:a0 + n], scalar1=w1, scalar2=None, op0=ALU.mult),
                     ins=[b_S1[tb], b_par], outs=[b_S3[tb]])
                P.op(dve, lambda e: e.scalar_tensor_tensor(out=acc, in0=S1[:, a0 - 1:a0 - 1 + n], scalar=w0, in1=acc, op0=ALU.mult, op1=ALU.add),
                     ins=nb(b_S1, tb) + [b_S3[tb], b_par], outs=[b_S3[tb]])
                P.op(dve, lambda e: e.scalar_tensor_tensor(out=acc, in0=S1[:, a0 + 1:a0 + 1 + n], scalar=w2, in1=acc, op0=ALU.mult, op1=ALU.add),
                     ins=nb(b_S1, tb) + [b_S3[tb], b_par], outs=[b_S3[tb]])
                P.op(dve, lambda e: e.tensor_tensor(out=yTh[:, cc, t0:t0 + n], in0=acc, in1=S2[:, a0:a0 + n], op=ALU.mult),
                     ins=[b_S3[tb], b_S2[tb]], outs=[b_y[cc][tb]])

            def pool_evac(cc, sl, tb):
                t0, n = TBS[tb]
                a0 = col0(tb)
                pp, bpp = proj_fm(sl[0], tb)
                P.op(act, lambda e: e.activation(out=S1[:, a0:a0 + n], in_=pp[:, :n], func=AF.Identity), ins=[bpp], outs=[b_S1[tb]])

            def pool_tail(cc, tb):
                t0, n = TBS[tb]
                a0 = col0(tb)
                b0 = a0 + n

                def shadd(dst, bd, src, bs, lo_, hi_, sh):
                    P.op(dve, lambda e: e.tensor_tensor(out=dst[:, lo_:hi_], in0=src[:, lo_ + sh:hi_ + sh], in1=src[:, lo_ - sh:hi_ - sh], op=ALU.add),
                         ins=nb(bs, tb), outs=nb(bd, tb))
                P.op(dve, lambda e: e.tensor_tensor(out=S2[:, a0 - 7:b0 + 7], in0=S1[:, a0 - 7:b0 + 7], in1=S1[:, a0 - 8:b0 + 6], op=ALU.add),
                     ins=nb(b_S1, tb), outs=nb(b_S2, tb))
                shadd(S3, b_S3, S2, b_S2, a0 - 6, b0 + 6, 1)
                if cc == 1:
                    shadd(S2, b_S2, S3, b_S3, a0 - 4, b0 + 4, 2)
                    shadd(S3, b_S3, S2, b_S2, a0, b0, 4)
                for half, (Pb, bP) in enumerate(((S2, b_S2), (S3, b_S3))):
                    rows = slice(half * 64, (half + 1) * 64)
                    fixes = []
                    if tb in (0, 4):
                        fixes.append((a0, par[rows, P_EC + cc * 16:P_EC + cc * 16 + 8]))
                    if tb in (3, 4):
                        fixes.append((b0 - 8, par[rows, P_EC + cc * 16 + 8:P_EC + cc * 16 + 16]))
                    for (c_, ecap) in fixes:
                        P.op(dve, lambda e, c_=c_, ecap=ecap, Pb=Pb, rows=rows: e.tensor_tensor(
                            out=Pb[rows, c_:c_ + 8], in0=Pb[rows, c_:c_ + 8], in1=ecap, op=ALU.mult), ins=[bP[tb], b_par], outs=[bP[tb]])
                    P.op(dve, lambda e, Pb=Pb, rows=rows: e.scalar_tensor_tensor(
                        out=Dt[rows, t0:t0 + n], in0=Pb[rows, a0:b0], scalar=par[rows, P_INVW + cc:P_INVW + cc + 1], in1=S1[rows, a0:b0],
                        op0=ALU.mult, op1=ALU.subtract), ins=[bP[tb], b_S1[tb], b_par], outs=[b_D[tb]])
                pp, bpp = ps_s()
                mm_group(pp[:, :n], [(pbd[:, cc, :], Dt[:, t0:t0 + n])], ins=[b_pbd, b_D[tb]], outb=bpp)
                P.op(act, lambda e: e.activation(out=yTh[:, 2 + cc, t0:t0 + n], in_=pp[:, :n], func=AF.Identity, scale=pcol(lo + 102 + cc)),
                     ins=[bpp, b_par], outs=[b_y[2 + cc][tb]])

            units = [(pool_evac, pool_tail, 0, (0,), (6,)), (pool_evac, pool_tail, 1, (1,), (7,)),
                     (conv_evac, conv_tail, 0, (2, 3, 4), (0, 2, 4)), (conv_evac, conv_tail, 1, (5, 0, 1), (1, 3, 5))]
            for (_, _, _, sl, cchs) in units[:3]:
                for s_, c_ in zip(sl, cchs):
                    load_win(s_, c_)
            for u in range(len(units) + 1):
                if u == 2:
                    for s_, c_ in zip(units[3][3], units[3][4]):
                        load_win(s_, c_)
                prev = units[u - 1] if u >= 1 else None
                curu = units[u] if u < len(units) else None
                if prev is not None:
                    prev[1](prev[2], xb[0])
                for bi, tb in enumerate(xb):
                    if prev is not None and bi + 1 < len(xb):
                        prev[1](prev[2], xb[bi + 1])
                    if curu is not None:
                        curu[0](curu[2], curu[3], tb)

            def wout_pass(hf):
                P.dma(pool, [(W2R[:, 0:4096], WOUT[l, hf])], outs=[b_w2], dbuf=b_w2)
                hook = next_norm if hf == 1 else None
                for bi, tb in enumerate(xb):
                    t0, n = TBS[tb]
                    wh = 1 if tb == 4 else 0
                    for oc in range(8):
                        if hook is not None and bi >= 1 and oc == 4:
                            hook(xb[bi - 1])
                        po, bo = ps_s()
                        pairs = [(W2R[:, c * 1024 + oc * 128:c * 1024 + (oc + 1) * 128], yTh[:, c, t0:t0 + n]) for c in range(4)]
                        mm_group(po[:, :n], pairs, ins=[b_w2] + [b_y[c][tb] for c in range(4)], outb=bo)
                        P.op(dve, lambda e, po=po, oc=oc, t0=t0, n=n, wh=wh: e.scalar_tensor_tensor(
                            out=xT[:, oc, t0:t0 + n], in0=po[:, :n], scalar=DERL[l][:, 1, wh, 2, oc:oc + 1], in1=xT[:, oc, t0:t0 + n],
                            op0=ALU.mult, op1=ALU.add), ins=[bo, b_hgl[l][1], b_x[oc][tb]], outs=[b_x[oc][tb]])
                if hook is not None:
                    hook(xb[-1])

            if dbg:
                P.dma(pool, [(DBG[:, 0:4, :], yTh[:, :, :])], ins=[b_y[c][t] for c in range(4) for t in range(5)], dbuf=b_out)
            wout_pass(0)

            QT0 = rb(S1o, T)
            KT = rb(S1o + T, T)
            Vp = rb(S2o, 18 * 192).rearrange("p (t c) -> p t c", c=192)
            QT1 = rb(S3o, T)
            WTi = rb(S3o + T, 1792).rearrange("p (h c) -> p h c", c=896)
            QTs = (QT0, QT1)
            b_Q = [Buf("Q%d" % t) for t in range(5)]
            b_K = [Buf("K%d" % t) for t in range(5)]
            b_V = [Buf("V%d" % t) for t in range(5)]
            b_const = Buf("qzero_vones")
            first_pair = True
            for p in range(4):
                sq_, sk_, sv_ = [(2 + 3 * p + i_) % 6 for i_ in range(3)]
                load_win(sq_, 8 + p)
                load_win(sk_, 12 + p)
                load_win(sv_, 16 + p)
                if first_pair:
                    P.barrier()
                    first_pair = False
                    P.op(dve, lambda e: e.memset(QT0[64:128, :], 0.0), outs=[b_const])
                    P.op(dve, lambda e: e.memset(QT1[0:64, :], 0.0), outs=[b_const])
                    P.op(dve, lambda e: e.memset(Vp[:, :, 64:128], 1.0), outs=[b_const])
                P.dma(pool, [(WTp[:, hh, :], WT[l, 2 * p + hh, 0]) for hh in range(2)] + [(WTi[:, hh, :], WT[l, 2 * p + hh, 1]) for hh in range(2)],
                      outs=[b_WTp], dbuf=b_WTp, phase=True)
                for tb in xb:
                    t0, n = TBS[tb]
                    pp, bpp = proj_fm(sq_, tb)
                    P.op(act, lambda e, pp=pp, t0=t0, n=n: e.activation(out=QT0[0:64, t0:t0 + n], in_=pp[0:64, :n], func=AF.Identity, scale=0.125),
                         ins=[bpp], outs=[b_Q[tb]])
                    P.op(act, lambda e, pp=pp, t0=t0, n=n: e.activation(out=QT1[64:128, t0:t0 + n], in_=pp[64:128, :n], func=AF.Identity, scale=0.125),
                         ins=[bpp], outs=[b_Q[tb]])
                for tb in range(5):
                    t0, n = TBS[tb]
                    pp, bpp = proj_fm(sk_, tb)
                    P.op(dve, lambda e, pp=pp, t0=t0, n=n: e.tensor_copy(out=KT[:, t0:t0 + n], in_=pp[:, :n]), ins=[bpp], outs=[b_K[tb]])
                wv, wb = ws_win(sv_)
                for tb in range(5):
                    t0, n = TBS[tb]
                    nt = n // 128
                    pp, bpp = ps_s()
                    for ti in range(nt):
                        tt = t0 // 128 + ti
                        mm_group(pp[:, ti * 128:(ti + 1) * 128],
                                 [(hT[:, kc, tt * 128:(tt + 1) * 128], wv[:, kc, :]) for kc in range(8)], ins=wb + [b_h[tb]], outb=bpp)
                    for hh in range(2):
                        P.op(dve, lambda e, pp=pp, t0=t0, nt=nt, n=n, hh=hh: e.tensor_copy(
                            out=Vp[:, t0 // 128:t0 // 128 + nt, hh * 128:hh * 128 + 64],
                            in_=pp[:, :n].rearrange("p (t c) -> p t c", c=128)[:, :, hh * 64:(hh + 1) * 64]),
                            ins=[bpp], outs=[b_V[tb]])

                qblocks = []
                for m in range(8):
                    if m == 0:
                        ccs = [2, 3, 4, 5]
                    elif m == 7:
                        ccs = [0, 1, 2, 3]
                    else:
                        ccs = [0, 1, 2, 3, 4, 5]
                    interior = (1 <= m <= 6)
                    chunks = [(2 * m - 2 + c, c, interior) for c in ccs] + [(16, None, False), (17, None, False)]
                    qblocks.append((256 * m, chunks))
                if not last:
                    qblocks.append((2048, [(16, None, False), (17, None, False)]))

                LAG = 3
                pending = []

                def flush(keep):
                    while len(pending) > keep:
                        pending.pop(0)()

                for (q0, chunks) in qblocks:
                    qtb = q0 // 512
                    po, bo = ps_l()
                    for hh in range(2):
                        npair = len(chunks) // 2
                        for cp in range(npair):
                            pS, bS = ps_s()
                            for ci in range(2):
                                kci, c, interior = chunks[2 * cp + ci]
                                ktb = (kci * 128) // 512
                                oap = pS[:, ci * 256:(ci + 1) * 256]
                                pairs = [(KT[:, kci * 128:(kci + 1) * 128], QTs[hh][:, q0:q0 + 256])]
                                ins_ = [b_K[ktb], b_Q[qtb], b_const]
                                if c is not None:
                                    Wsel = WTi if interior else WTp
                                    pairs.append((ident, Wsel[:, hh, (10 - 2 * c) * 64:(10 - 2 * c) * 64 + 256]))
                                    ins_ += [b_cb, b_WTp]
                                mm_group(oap, pairs, ins=ins_, outb=bS)
                            pi = st.get("pt", 0) % NPT
                            st["pt"] = pi + 1
                            P.op(act, lambda e, pS=pS, pi=pi: e.activation(out=PT[pi], in_=pS[:, :], func=AF.Exp),
                                 ins=[bS], outs=[b_PT[pi]])

                            def pv_job(cp=cp, pi=pi, hh=hh, chunks=chunks, po=po, bo=bo, npair=npair):
                                kcis = [chunks[2 * cp + ci][0] for ci in range(2)]
                                first_ = (hh == 0 and cp == 0)
                                last_ = (cp == npair - 1)

                                def fn(e):
                                    r = None
                                    for ci in range(2):
                                        r = e.matmul(po[:, hh * 256:(hh + 1) * 256], Vp[:, kcis[ci], hh * 64:hh * 64 + 128],
                                                     PT[pi][:, ci * 256:(ci + 1) * 256], start=(first_ and ci == 0), stop=(last_ and ci == 1),
                                                     skip_group_check=True)
                                    return r
                                P.op(pe, fn, ins=[b_V[(k * 128) // 512] for k in kcis] + [b_PT[pi], b_const], outs=[bo])
                            pending.append(pv_job)
                            flush(LAG)

                    def evac_job(po=po, bo=bo, q0=q0, p=p):
                        ri = st.get("rd", 0) % 2
                        st["rd"] = ri + 1
                        tb_ = q0 // 512
                        P.op(dve, lambda e: e.reciprocal(out=rden[ri][0:64, :], in_=po[64:128, 0:256]), ins=[bo], outs=[b_rden[ri]])
                        P.op(dve, lambda e: e.reciprocal(out=rden[ri][64:128, :], in_=po[0:64, 256:512]), ins=[bo], outs=[b_rden[ri]])
                        P.op(dve, lambda e: e.tensor_tensor(
                            out=yTh[0:64, p, q0:q0 + 256], in0=po[0:64, 0:256], in1=rden[ri][0:64, :], op=ALU.mult),
                            ins=[bo, b_rden[ri]], outs=[b_y[p][tb_]])
                        P.op(dve, lambda e: e.tensor_tensor(
                            out=yTh[64:128, p, q0:q0 + 256], in0=po[64:128, 256:512], in1=rden[ri][64:128, :], op=ALU.mult),
                            ins=[bo, b_rden[ri]], outs=[b_y[p][tb_]])
                    pending.append(evac_job)
                flush(0)
            if dbg:
                P.dma(pool, [(DBG[:, 4:8, :], yTh[:, :, :])], ins=[b_y[c][t] for c in range(4) for t in range(5)], dbuf=b_out)
            wout_pass(1)

        hflat = hT[:, :, :].rearrange("p k t -> p (k t)")
        ot = [hflat[:, i_ * 8192:(i_ + 1) * 8192].bitcast(F32).rearrange("p (k t) -> p k t", t=512) for i_ in range(2)]

        def final_norm(tb):
            t0, n = TBS[tb]
            oi = tb % 2
            norm(tb, lambda kc: par[:, P_FG + kc:P_FG + kc + 1], None, lambda kc: ot[oi][:, kc, :], list(b_h), [b_par])
            P.dma(sp, [(OUT[:, :, t0:t0 + n], ot[oi][:, :, :])], ins=list(b_h), dbuf=b_out)

        make_mod_items(layer_list[0])
        pump_mods(8)
        for s_ in range(3):
            if b_wm[s_].w is not None:
                sp.wait(b_wm[s_].w)
        for tb in range(1, 5):
            load_x(tb)
        nl = len(layer_list)
        stopped = False
        for li, l in enumerate(layer_list):
            last = (l == DEPTH - 1)
            cur["l"] = l
            full = stop is None
            ffn(l, 0, [0, 1, 2, 3, 4], pre_normed=(li > 0),
                next_norm=(lambda tb, l=l: mod_norm(tb, 1, l)) if stop != "ffn1" else None)
            pump_mods(100)
            if stop == "ffn1":
                stopped = True
                break
            blocks2 = [0, 1, 2, 3] if last else [0, 1, 2, 3, 4]
            mixer(l, last, pre_normed=True, next_norm=(lambda tb, l=l: mod_norm(tb, 2, l)) if stop != "mixer" else None)
            if stop == "mixer":
                stopped = True
                break
            if li + 1 < nl:
                make_mod_items(layer_list[li + 1])
                nn = (lambda tb, l2=layer_list[li + 1]: mod_norm(tb, 0, l2))
            elif do_final:
                nn = final_norm
            else:
                nn = None
            ffn(l, 1, blocks2, pre_normed=True, next_norm=nn)
            pump_mods(100)

        if not do_final:
            P.barrier()
            for tb, (t0, n) in enumerate(TBS):
                P.dma(sp, [(OUT[:, :, t0:t0 + n], xT[:, :, t0:t0 + n])], ins=[b_x[k][tb] for k in range(8)], dbuf=b_out)
        sp.wait((b_out.sem, b_out.semv))
        if dbg:
            pool.wait((b_out.sem, b_out.semv))

        with nc.Block() as block:
            @block.tensor
            def _(e):
                for f_ in pe.ops:
                    f_(e)

            @block.scalar
            def _(e):
                for f_ in act.ops:
                    f_(e)

            @block.vector
            def _(e):
                for f_ in dve.ops:
                    f_(e)

            @block.gpsimd
            def _(e):
                for f_ in pool.ops:
                    f_(e)

            @block.sync
            def _(e):
                for f_ in sp.ops:
                    f_(e)
    return nc


_CACHE = {}


def _run(nc_key, builder, in_maps):
    if nc_key not in _CACHE:
        _CACHE[nc_key] = builder()
    return run_bass_kernel_spmd(_CACHE[nc_key], in_maps, core_ids=list(range(8)))


def kernel(**inputs):
    shared = _host_shared(inputs)
    in_maps = []
    for b in range(8):
        m = dict(shared)
        m["xt"] = _host_xt(inputs, b)
        m["par"] = _host_params(inputs, b)
        in_maps.append(m)
    res = _run("fused", lambda: build((0, 1), True), in_maps)
    out = np.empty((8, S, D), np.float32)
    for b in range(8):
        o = res.results[b]["out"]
        out[b] = o.transpose(2, 1, 0).reshape(S, D)
    return out
```
